# Optimizing a Trainium2 kernel written in Bass

```python
import math
import jax, jax.numpy as jnp
from jax import lax
import numpy as np

D_MODEL = 1024
BATCH = 4
SEQ = 4096
DEPTH = 1

POOL_WINDOWS = (2, 4, 8, 16)
POOL_GROUPS = 4
POOL_GROUP_DIM = D_MODEL // 8
POOL_DIM = POOL_GROUPS * POOL_GROUP_DIM
N_HEADS = 8
HEAD_DIM = 64
ATTN_DIM = N_HEADS * HEAD_DIM
IDX_HEADS = 8
IDX_DIM = 64
TOP_K = 256
Q_BLOCK = 128
REL_BUCKETS = 32
REL_MAX_DIST = 128
D_FF = 2816
N_ADA = 9
LN_EPS = 1e-5
DEEPNORM_ALPHA = (2.0 * DEPTH) ** 0.25
DEEPNORM_BETA = (8.0 * DEPTH) ** -0.25
IN_SPLITS = (POOL_DIM, ATTN_DIM, ATTN_DIM, ATTN_DIM, IDX_HEADS * IDX_DIM, IDX_DIM, IDX_HEADS, D_MODEL, D_MODEL)
IN_COLS = sum(IN_SPLITS)
V_OFFSET = POOL_DIM + 2 * ATTN_DIM

kernel_name = "hybrid_pool_dsa_macaron_block"


def layer_norm(h, g, b):
    h32 = h.astype(jnp.float32)
    mu = jnp.mean(h32, axis=-1, keepdims=True)
    var = jnp.mean(jnp.square(h32 - mu), axis=-1, keepdims=True)
    return ((h32 - mu) * lax.rsqrt(var + LN_EPS) * g.astype(jnp.float32) + b.astype(jnp.float32)).astype(h.dtype)


def swiglu(u, w_gate, w_up, w_down):
    return (jax.nn.silu(u @ w_gate) * (u @ w_up)) @ w_down


def t5_bucket(n):
    max_exact = REL_BUCKETS // 2
    nf = jnp.maximum(n, 1).astype(jnp.float32)
    large = max_exact + (jnp.log(nf / max_exact) / math.log(REL_MAX_DIST / max_exact)
                         * (REL_BUCKETS - max_exact)).astype(jnp.int32)
    large = jnp.minimum(large, REL_BUCKETS - 1)
    return jnp.where(n < max_exact, n, large)


def pool_mixer(p, w_pool, pool_scale):
    B, S, _ = p.shape
    p32 = p.astype(jnp.float32)
    prefix = jnp.pad(jnp.cumsum(p32, axis=1), ((0, 0), (1, 0), (0, 0)))
    t = jnp.arange(S)
    outs = []
    for g, w in enumerate(POOL_WINDOWS):
        sl = slice(g * POOL_GROUP_DIM, (g + 1) * POOL_GROUP_DIM)
        pre = prefix[:, :, sl]
        hi = pre[:, 1:]
        lo = jnp.pad(pre[:, :S + 1 - w], ((0, 0), (w - 1, 0), (0, 0)))
        cnt = jnp.minimum(t + 1, w).astype(jnp.float32)[None, :, None]
        outs.append((hi - lo) / cnt - p32[:, :, sl])
    pooled = jnp.stack(outs, axis=2).astype(p.dtype)
    mixed = jnp.einsum('bsgc,gcd->bsgd', pooled, w_pool).reshape(B, S, POOL_DIM)
    return mixed * pool_scale


def dsa_attention(q, k, v, qi, ki, wi, rel_bias):
    B, S = q.shape[0], q.shape[1]
    topk = min(TOP_K, S // 4)
    n_blocks = S // Q_BLOCK
    key_pos = jnp.arange(S)

    def block(i):
        start = i * Q_BLOCK
        qb = lax.dynamic_slice_in_dim(q, start, Q_BLOCK, axis=1)
        qib = lax.dynamic_slice_in_dim(qi, start, Q_BLOCK, axis=1)
        wib = lax.dynamic_slice_in_dim(wi, start, Q_BLOCK, axis=1)
        t_pos = start + jnp.arange(Q_BLOCK)
        causal = key_pos[None, :] <= t_pos[:, None]
        rel = jax.nn.relu(jnp.einsum('bqhd,bsd->bqhs', qib, ki).astype(jnp.float32) * (IDX_DIM ** -0.5))
        score = jnp.einsum('bqhs,bqh->bqs', rel, wib.astype(jnp.float32)) * (IDX_HEADS ** -0.5)
        score = jnp.where(causal[None], score, -jnp.inf)
        _, idx = lax.top_k(score, topk)
        k_sel = jax.vmap(lambda kb, ib: kb[ib])(k, idx)
        v_sel = jax.vmap(lambda vb, ib: vb[ib])(v, idx)
        dist = t_pos[None, :, None] - idx
        valid = dist >= 0
        bias = rel_bias[t5_bucket(jnp.maximum(dist, 0))]
        logits = (jnp.einsum('bqhd,bqkhd->bqhk', qb, k_sel).astype(jnp.float32) * (HEAD_DIM ** -0.5)
                  + jnp.transpose(bias, (0, 1, 3, 2)).astype(jnp.float32))
        logits = jnp.where(valid[:, :, None, :], logits, -jnp.inf)
        probs = jax.nn.softmax(logits, axis=-1).astype(v.dtype)
        return jnp.einsum('bqhk,bqkhd->bqhd', probs, v_sel)

    out = lax.map(block, jnp.arange(n_blocks))
    return jnp.transpose(out, (1, 0, 2, 3, 4)).reshape(B, S, N_HEADS * HEAD_DIM)


def token_mixer(u, w_in, w_pool, pool_scale, w_a, w_b, w_out, rel_bias):
    B, S, _ = u.shape
    proj = u @ w_in
    p, q, k, v, qi, ki, wi, ga, gb = jnp.split(proj, np.cumsum(IN_SPLITS)[:-1].tolist(), axis=-1)
    y_a = pool_mixer(p, w_pool, pool_scale) @ w_a
    hs = (B, S, N_HEADS, HEAD_DIM)
    y_b = dsa_attention(q.reshape(hs), k.reshape(hs), v.reshape(hs),
                        qi.reshape(B, S, IDX_HEADS, IDX_DIM), ki, wi, rel_bias) @ w_b
    merged = jax.nn.sigmoid(ga) * y_a + jax.nn.sigmoid(gb) * y_b
    return merged @ w_out


def setup_inputs(seed: int = 0) -> dict:
    key = jax.random.key(seed)
    ks = jax.random.split(key, 22)
    L = DEPTH

    def nrm(k, shape, scale):
        return jax.random.normal(k, shape, jnp.float32) * scale

    w_in = nrm(ks[10], (L, D_MODEL, IN_COLS), D_MODEL ** -0.5)
    w_in = w_in.at[:, :, V_OFFSET:V_OFFSET + ATTN_DIM].multiply(DEEPNORM_BETA)
    return {
        "x": nrm(ks[0], (BATCH, SEQ, D_MODEL), 1.0),
        "c": nrm(ks[1], (BATCH, D_MODEL), 1.0),
        "w_ada": nrm(ks[2], (L, D_MODEL, N_ADA * D_MODEL), 0.5 * D_MODEL ** -0.5),
        "b_ada": nrm(ks[3], (L, N_ADA * D_MODEL), 0.02),
        "ln_g": 1.0 + nrm(ks[4], (L, 3, D_MODEL), 0.02),
        "ln_b": nrm(ks[5], (L, 3, D_MODEL), 0.02),
        "ffn1_w_gate": nrm(ks[6], (L, D_MODEL, D_FF), D_MODEL ** -0.5),
        "ffn1_w_up": nrm(ks[7], (L, D_MODEL, D_FF), D_MODEL ** -0.5),
        "ffn1_w_down": nrm(ks[8], (L, D_FF, D_MODEL), DEEPNORM_BETA * D_FF ** -0.5),
        "w_in": w_in,
        "w_pool": nrm(ks[11], (L, POOL_GROUPS, POOL_GROUP_DIM, POOL_GROUP_DIM), POOL_GROUP_DIM ** -0.5),
        "pool_scale": 1.0 + nrm(ks[12], (L, POOL_DIM), 0.02),
        "w_a": nrm(ks[13], (L, POOL_DIM, D_MODEL), POOL_DIM ** -0.5),
        "w_b": nrm(ks[14], (L, ATTN_DIM, D_MODEL), ATTN_DIM ** -0.5),
        "w_out": nrm(ks[15], (L, D_MODEL, D_MODEL), DEEPNORM_BETA * D_MODEL ** -0.5),
        "rel_bias": nrm(ks[16], (REL_BUCKETS, N_HEADS), 0.5),
        "ffn2_w_gate": nrm(ks[17], (L, D_MODEL, D_FF), D_MODEL ** -0.5),
        "ffn2_w_up": nrm(ks[18], (L, D_MODEL, D_FF), D_MODEL ** -0.5),
        "ffn2_w_down": nrm(ks[19], (L, D_FF, D_MODEL), DEEPNORM_BETA * D_FF ** -0.5),
    }


def reference(x, c, w_ada, b_ada, ln_g, ln_b, ffn1_w_gate, ffn1_w_up, ffn1_w_down,
              w_in, w_pool, pool_scale, w_a, w_b, w_out, rel_bias,
              ffn2_w_gate, ffn2_w_up, ffn2_w_down):
    B = x.shape[0]
    for l in range(DEPTH):
        mod = (jax.nn.silu(c) @ w_ada[l] + b_ada[l]).reshape(B, N_ADA, 1, D_MODEL)
        sh1, sc1, g1, sh2, sc2, g2, sh3, sc3, g3 = [mod[:, j] for j in range(N_ADA)]
        u = x * (1.0 + sc1) + sh1
        x = layer_norm(DEEPNORM_ALPHA * x + 0.5 * g1 * swiglu(u, ffn1_w_gate[l], ffn1_w_up[l], ffn1_w_down[l]),
                       ln_g[l, 0], ln_b[l, 0])
        u = x * (1.0 + sc2) + sh2
        y = token_mixer(u, w_in[l], w_pool[l], pool_scale[l], w_a[l], w_b[l], w_out[l], rel_bias)
        x = layer_norm(DEEPNORM_ALPHA * x + g2 * y, ln_g[l, 1], ln_b[l, 1])
        u = x * (1.0 + sc3) + sh3
        x = layer_norm(DEEPNORM_ALPHA * x + 0.5 * g3 * swiglu(u, ffn2_w_gate[l], ffn2_w_up[l], ffn2_w_down[l]),
                       ln_g[l, 2], ln_b[l, 2])
    return x
```

```python
import math
import os
import numpy as np
import concourse.bass as bass
import concourse.mybir as mybir
from concourse.bass_utils import run_bass_kernel_spmd
from contextlib import ExitStack

F32 = mybir.dt.float32
BF16 = mybir.dt.bfloat16
AF = mybir.ActivationFunctionType
ALU = mybir.AluOpType
AX = mybir.AxisListType

D = 1024
S_LEN = 4096
NOWN = 2048
NTOK = 4352
DFF = 2816
NF = 22
INC = 4680
ALPHA = 2.0 ** 0.25
EPS_P = 1e-5 / (ALPHA * ALPHA)
NEG = -30000.0
XR = 4096.0
NBIS = 22
NDS = 32
STAGES = 9
DEBUG = False


class Res:
    __slots__ = ("w", "r")

    def __init__(self):
        self.w = None
        self.r = []


class Sched:
    def __init__(self, nc, es):
        self.nc = nc
        self.E = {}
        for name, h in [("pe", nc.tensor), ("act", nc.scalar), ("dve", nc.vector),
                        ("pool", nc.gpsimd), ("sp", nc.sync)]:
            sem = es.enter_context(nc.semaphore("s_" + name))
            self.E[name] = dict(h=h, sem=sem, n=0, seen={})
        self.dsems = {q: [es.enter_context(nc.semaphore(f"dq{q}{i}")) for i in range(NDS)] for q in ("sp", "pool")}
        self.dn = {"sp": 0, "pool": 0}
        self.dtoks = {"sp": [None] * NDS, "pool": [None] * NDS}

    def _wait(self, e, tok):
        if tok is None:
            return
        sem, val, owner = tok
        E = self.E[e]
        if owner == e and e in ("pe", "sp"):
            return
        key = sem.num
        if E["seen"].get(key, 0) >= val:
            return
        E["h"].wait_ge(sem, val)
        E["seen"][key] = val

    def _deps(self, e, reads, writes):
        for r in reads:
            self._wait(e, r.w)
        for w in writes:
            self._wait(e, w.w)
            for t in w.r:
                self._wait(e, t)

    def _mark(self, tok, reads, writes):
        for r in reads:
            r.r.append(tok)
            if len(r.r) > 48:
                best = {}
                for t in r.r:
                    k = t[0].num
                    if k not in best or best[k][1] < t[1]:
                        best[k] = t
                r.r = list(best.values())
        for w in writes:
            w.w = tok
            w.r = []

    def op(self, e, fn, reads=(), writes=()):
        self._deps(e, reads, writes)
        E = self.E[e]
        ins = fn(E["h"])
        E["n"] += 1
        ins.then_inc(E["sem"], 1)
        tok = (E["sem"], E["n"], e)
        self._mark(tok, reads, writes)
        return tok

    def dma(self, e, out, in_, reads=(), writes=()):
        i = self.dn[e] % NDS
        self.dn[e] += 1
        self._wait(e, self.dtoks[e][i])
        self._deps(e, reads, writes)
        E = self.E[e]
        ins = E["h"].dma_start(out=out, in_=in_)
        val = 16 * ((self.dn[e] - 1) // NDS + 1)
        ins.then_inc(self.dsems[e][i], 16)
        tok = (self.dsems[e][i], val, None)
        self.dtoks[e][i] = tok
        self._mark(tok, reads, writes)
        return tok

    def barrier(self):
        toks = [(E["sem"], E["n"], n) for n, E in self.E.items() if E["n"] > 0]
        toks += [t for q in self.dtoks.values() for t in q if t is not None]
        for e in self.E:
            for t in toks:
                if t[2] == e:
                    continue
                self._wait(e, t)


def build(debug=False, stages=9):
    nc = bass.Bass("TRN2", target_bir_lowering=False)

    def din(name, shape, dt=F32):
        return nc.dram_tensor(name, shape, dt, kind="ExternalInput").ap()

    xT = din("xT", [128, 8, NTOK])
    c_fm = din("c_fm", [128, 8])
    bada = din("bada", [128, 72])
    lng = din("lng", [128, 24])
    lnb = din("lnb", [128, 24])
    psc = din("psc", [128, 4])
    w_ada = din("w_ada", [D, 9 * D])
    f1g = din("f1g", [D, DFF])
    f1u = din("f1u", [D, DFF])
    f1d = din("f1d", [DFF, D])
    w_in = din("w_in", [D, INC])
    w_pool = din("w_pool", [4, 128, 128])
    w_a = din("w_a", [512, D])
    w_b = din("w_b", [512, D])
    w_out = din("w_out", [D, D])
    f2g = din("f2g", [D, DFF])
    f2u = din("f2u", [D, DFF])
    f2d = din("f2d", [DFF, D])
    sp_in = din("sp_in", [128, 8, 3, 128])
    im_in = din("im_in", [128, 2, 128])
    b31_in = din("b31_in", [128, 8])
    hval_in = din("hval_in", [128, 256])
    corr_in = din("corr_in", [128, 4, 128])
    outT = nc.dram_tensor("outT", [128, 8, NOWN], F32, kind="ExternalOutput").ap()
    okind = "ExternalOutput" if debug else "Internal"
    x1sp = nc.dram_tensor("x1sp", [128, 8, NOWN], F32, kind=okind).ap()
    u2sp = nc.dram_tensor("u2sp", [128, 8, NOWN], BF16, kind=okind).ap()
    if debug:
        dbgK = nc.dram_tensor("dbgK", [128, 4, 4096], BF16, kind="ExternalOutput").ap()
        dbgV = nc.dram_tensor("dbgV", [128, 32, 520], BF16, kind="ExternalOutput").ap()
        dbgS = nc.dram_tensor("dbgS", [128, 4096], F32, kind="ExternalOutput").ap()
        dbgM = nc.dram_tensor("dbgM", [128, 72], F32, kind="ExternalOutput").ap()

    w_ada_r = w_ada.rearrange("(k p) n -> p k n", p=128)
    w_in_r = w_in.rearrange("(k p) n -> p k n", p=128)
    w_out_r = w_out.rearrange("(k p) n -> p k n", p=128)

    with ExitStack() as es:
        S = Sched(nc, es)

        def sbt(stack, name, shape, dt):
            return stack.enter_context(nc.sbuf_tensor(name, shape, dt))

        banks = [es.enter_context(nc.psum_tensor(f"bank{i}", [128, 512], F32)) for i in range(7)]
        ptb = es.enter_context(nc.psum_tensor("ptb", [128, 1024], BF16))
        Rb = [Res() for _ in range(7)]
        Rpt = Res()

        modv = sbt(es, "modv", [128, 72], F32)
        vecs = sbt(es, "vecs", [128, 12, 8], F32)
        lng_t = sbt(es, "lng_t", [128, 24], F32)
        lnb_t = sbt(es, "lnb_t", [128, 24], F32)
        psc_t = sbt(es, "psc_t", [128, 4], F32)
        onesM = sbt(es, "onesM", [128, 128], BF16)
        ident = sbt(es, "ident", [128, 128], BF16)
        phalo = sbt(es, "phalo", [128, 4, 256], F32)
        Rc = Res()
        Rphalo = Res()
        V_A1, V_SH1, V_C1, V_G2, V_B2, V_C2, V_G3, V_B3, V_C3, V_T0, V_T1, V_T2 = range(12)

        S.dma("sp", lng_t[:], lng, writes=[Rc])
        S.dma("sp", lnb_t[:], lnb, writes=[Rc])
        S.dma("sp", psc_t[:], psc, writes=[Rc])
        S.op("dve", lambda h: h.memset(onesM[:], 1.0 / 1024.0), writes=[Rc])
        S.op("pool", lambda h: h.memset(ident[:], 1.0), writes=[Rc])
        S.op("pool", lambda h: h.affine_select(out=ident[:], in_=ident[:], pattern=[[-1, 128]],
                                               compare_op=ALU.is_equal, fill=0.0, base=0,
                                               channel_multiplier=1), reads=[Rc], writes=[Rc])

        with ExitStack() as s0:
            cf = sbt(s0, "cf", [128, 8], F32)
            csl = sbt(s0, "csl", [128, 8], BF16)
            bad = sbt(s0, "bad", [128, 72], F32)
            wab = [sbt(s0, f"wab{i}", [128, 8, 1024], BF16) for i in range(2)]
            Rwab = [Res(), Res()]
            Rcf = Res()
            S.dma("sp", cf[:], c_fm, writes=[Rcf])
            S.dma("sp", bad[:], bada, writes=[Rcf])
            S.op("act", lambda h: h.activation(out=csl[:], in_=cf[:], func=AF.Silu), reads=[Rcf], writes=[Rcf])
            for v in range(9):
                wb = wab[v % 2]
                S.dma("pool", wb[:], w_ada_r[:, :, v * 1024:(v + 1) * 1024], writes=[Rwab[v % 2]])
                for c in range(8):
                    j = v * 8 + c
                    for k in range(8):
                        S.op("pe", lambda h, wb=wb, c=c, k=k, j=j: h.matmul(
                            banks[6][:, j:j + 1], wb[:, k, c * 128:(c + 1) * 128], csl[:, k:k + 1],
                            start=(k == 0), stop=(k == 7)), reads=[Rwab[v % 2], Rcf], writes=[Rb[6]])
            S.op("dve", lambda h: h.tensor_tensor(out=modv[:], in0=banks[6][:, 0:72], in1=bad[:], op=ALU.add),
                 reads=[Rb[6], Rcf], writes=[Rc])

            def mv(i):
                return modv[:, i * 8:(i + 1) * 8]

            def vv(i):
                return vecs[:, i, :]

            def dv(fn, *a):
                S.op("dve", fn, reads=[Rc], writes=[Rc])
            dv(lambda h: h.tensor_scalar_add(out=vv(V_A1), in0=mv(1), scalar1=1.0))
            dv(lambda h: h.tensor_copy(out=vv(V_SH1), in_=mv(0)))
            dv(lambda h: h.tensor_scalar_mul(out=vv(V_C1), in0=mv(2), scalar1=0.5 / ALPHA))
            dv(lambda h: h.tensor_scalar_add(out=vv(V_T0), in0=mv(4), scalar1=1.0))
            dv(lambda h: h.tensor_tensor(out=vv(V_G2), in0=lng_t[:, 0:8], in1=vv(V_T0), op=ALU.mult))
            dv(lambda h: h.tensor_tensor(out=vv(V_T1), in0=lnb_t[:, 0:8], in1=vv(V_T0), op=ALU.mult))
            dv(lambda h: h.tensor_tensor(out=vv(V_B2), in0=vv(V_T1), in1=mv(3), op=ALU.add))
            dv(lambda h: h.tensor_scalar_mul(out=vv(V_C2), in0=mv(5), scalar1=1.0 / ALPHA))
            dv(lambda h: h.tensor_scalar_add(out=vv(V_T2), in0=mv(7), scalar1=1.0))
            dv(lambda h: h.tensor_tensor(out=vv(V_G3), in0=lng_t[:, 8:16], in1=vv(V_T2), op=ALU.mult))
            dv(lambda h: h.tensor_tensor(out=vv(V_T1), in0=lnb_t[:, 8:16], in1=vv(V_T2), op=ALU.mult))
            dv(lambda h: h.tensor_tensor(out=vv(V_B3), in0=vv(V_T1), in1=mv(6), op=ALU.add))
            dv(lambda h: h.tensor_scalar_mul(out=vv(V_C3), in0=mv(8), scalar1=0.5 / ALPHA))
            if debug:
                S.dma("sp", dbgM, modv[:], reads=[Rc])
            S.barrier()

        def vcol(i, c):
            return vecs[:, i, c:c + 1]

        def make_ffn_ctx(stack, pf):
            ctx = {}
            ctx["WA"] = [sbt(stack, pf + f"WA{i}", [128, 8, 128], BF16) for i in range(8)]
            ctx["RWA"] = [Res() for _ in range(8)]
            ctx["wa_n"] = 0
            ctx["WD"] = [sbt(stack, pf + f"WD{i}", [128, NF, 128], BF16) for i in range(3)]
            ctx["RWD"] = [Res() for _ in range(3)]
            ctx["wd_n"] = 0
            ctx["hT"] = sbt(stack, pf + "hT", [128, NF, 512], BF16)
            ctx["Rh"] = [Res() for _ in range(NF)]
            ctx["z"] = sbt(stack, pf + "z", [128, 8, 512], F32)
            ctx["Rz"] = [Res() for _ in range(8)]
            ctx["sg"] = [sbt(stack, pf + f"sg{i}", [128, 512], F32) for i in range(2)]
            ctx["Rsg"] = [Res(), Res()]
            ctx["sg_n"] = 0
            ctx["zb"] = [sbt(stack, pf + f"zb{i}", [128, 512], BF16) for i in range(2)]
            ctx["zs"] = [sbt(stack, pf + f"zs{i}", [128, 512], BF16) for i in range(2)]
            ctx["Rzb"] = [Res(), Res()]
            ctx["Rzs"] = [Res(), Res()]
            ctx["st"] = [sbt(stack, pf + f"st{i}", [128, 512], F32) for i in range(3)]
            ctx["Rst"] = Res()
            ctx["pb"] = 0
            return ctx

        def wa_load(ctx, src_ap):
            i = ctx["wa_n"] % 8
            ctx["wa_n"] += 1
            S.dma("pool", ctx["WA"][i][:], src_ap, writes=[ctx["RWA"][i]])
            return ctx["WA"][i], ctx["RWA"][i]

        def next_bank(ctx, lo, n):
            b = lo + (ctx["pb"] % n)
            ctx["pb"] += 1
            return b

        def ffn(ctx, uT, Ru, W, wg, wu, wd, cvec, resid_fn):
            hT, Rh, z, Rz = ctx["hT"], ctx["Rh"], ctx["z"], ctx["Rz"]
            wg_r = wg.rearrange("(k p) n -> p k n", p=128)
            wu_r = wu.rearrange("(k p) n -> p k n", p=128)
            wd_r = wd.rearrange("(k p) n -> p k n", p=128)
            for f in range(NF):
                tg, Rg = wa_load(ctx, wg_r[:, :, f * 128:(f + 1) * 128])
                tu, Ruu = wa_load(ctx, wu_r[:, :, f * 128:(f + 1) * 128])
                bg = f % 2
                bu = 2 + f % 2
                for k in range(8):
                    S.op("pe", lambda h, k=k: h.matmul(banks[bg][:, :W], tg[:, k, :], uT[:, k, :W],
                                                      start=(k == 0), stop=(k == 7)),
                         reads=[Rg, Ru], writes=[Rb[bg]])
                for k in range(8):
                    S.op("pe", lambda h, k=k: h.matmul(banks[bu][:, :W], tu[:, k, :], uT[:, k, :W],
                                                      start=(k == 0), stop=(k == 7)),
                         reads=[Ruu, Ru], writes=[Rb[bu]])
                si = ctx["sg_n"] % 2
                ctx["sg_n"] += 1
                sg, Rsg = ctx["sg"][si], ctx["Rsg"][si]
                S.op("act", lambda h: h.activation(out=sg[:, :W], in_=banks[bg][:, :W], func=AF.Silu),
                     reads=[Rb[bg]], writes=[Rsg])
                S.op("dve", lambda h: h.tensor_tensor(out=hT[:, f, :W], in0=banks[bu][:, :W], in1=sg[:, :W],
                                                      op=ALU.mult),
                     reads=[Rb[bu], Rsg], writes=[Rh[f]])
            for c in range(8):
                i = ctx["wd_n"] % 3
                ctx["wd_n"] += 1
                wdt, Rwd = ctx["WD"][i], ctx["RWD"][i]
                S.dma("pool", wdt[:], wd_r[:, :, c * 128:(c + 1) * 128], writes=[Rwd])
                resid_fn(c)
                by = 4 + c % 2
                for k in range(NF):
                    S.op("pe", lambda h, k=k: h.matmul(banks[by][:, :W], wdt[:, k, :], hT[:, k, :W],
                                                      start=(k == 0), stop=(k == NF - 1)),
                         reads=[Rwd, Rh[k]], writes=[Rb[by]])
                S.op("dve", lambda h: h.scalar_tensor_tensor(out=z[:, c, :W], in0=banks[by][:, :W],
                                                             scalar=vcol(cvec, c), in1=z[:, c, :W],
                                                             op0=ALU.mult, op1=ALU.add),
                     reads=[Rb[by], Rz[c], Rc], writes=[Rz[c]])

        def layernorm(ctx, W, outs):
            z, Rz = ctx["z"], ctx["Rz"]
            st, Rst = ctx["st"], ctx["Rst"]
            bm, bq = 6, 0
            for c in range(8):
                i = c % 2
                S.op("act", lambda h: h.activation(out=ctx["zb"][i][:, :W], in_=z[:, c, :W], func=AF.Copy),
                     reads=[Rz[c]], writes=[ctx["Rzb"][i]])
                S.op("act", lambda h: h.activation(out=ctx["zs"][i][:, :W], in_=z[:, c, :W], func=AF.Square),
                     reads=[Rz[c]], writes=[ctx["Rzs"][i]])
                S.op("pe", lambda h: h.matmul(banks[bm][:, :W], onesM[:], ctx["zb"][i][:, :W],
                                              start=(c == 0), stop=(c == 7)),
                     reads=[ctx["Rzb"][i], Rc], writes=[Rb[bm]])
                S.op("pe", lambda h: h.matmul(banks[bq][:, :W], onesM[:], ctx["zs"][i][:, :W],
                                              start=(c == 0), stop=(c == 7)),
                     reads=[ctx["Rzs"][i], Rc], writes=[Rb[bq]])
            S.op("act", lambda h: h.activation(out=st[0][:, :W], in_=banks[bm][:, :W], func=AF.Square),
                 reads=[Rb[bm]], writes=[Rst])
            S.op("dve", lambda h: h.tensor_tensor(out=st[0][:, :W], in0=banks[bq][:, :W], in1=st[0][:, :W],
                                                  op=ALU.subtract), reads=[Rb[bq], Rst], writes=[Rst])
            S.op("dve", lambda h: h.tensor_scalar_add(out=st[1][:, :W], in0=st[0][:, :W], scalar1=EPS_P),
                 reads=[Rst], writes=[Rst])
            S.op("act", lambda h: h.activation(out=st[1][:, :W], in_=st[1][:, :W], func=AF.Sqrt),
                 reads=[Rst], writes=[Rst])
            S.op("dve", lambda h: h.reciprocal(out=st[1][:, :W], in_=st[1][:, :W]), reads=[Rst], writes=[Rst])
            S.op("dve", lambda h: h.tensor_tensor(out=st[2][:, :W], in0=banks[bm][:, :W], in1=st[1][:, :W],
                                                  op=ALU.mult), reads=[Rb[bm], Rst], writes=[Rst])
            for c in range(8):
                S.op("dve", lambda h: h.tensor_tensor(out=z[:, c, :W], in0=z[:, c, :W], in1=st[1][:, :W],
                                                      op=ALU.mult), reads=[Rz[c], Rst], writes=[Rz[c]])
                S.op("dve", lambda h: h.tensor_tensor(out=z[:, c, :W], in0=z[:, c, :W], in1=st[2][:, :W],
                                                      op=ALU.subtract), reads=[Rz[c], Rst], writes=[Rz[c]])
                for f in outs:
                    f(c)

        attn_sp = nc.dram_tensor("attn_sp", [128, 4, NOWN], BF16, kind=okind).ap()
        q_sp = nc.dram_tensor("q_sp", [128, 4, NOWN], BF16, kind="Internal").ap()
        qi_sp = nc.dram_tensor("qi_sp", [128, 4, NOWN], BF16, kind="Internal").ap()
        with ExitStack() as skv:
            kT = sbt(skv, "kT", [128, 4, 4096], BF16)
            vS = sbt(skv, "vS", [128, 32, 8, 65], BF16)
            kiT = sbt(skv, "kiT", [128, 4096], BF16)
            RkT, RvS, RkiT = Res(), Res(), Res()
            S.op("dve", lambda h: h.memset(vS[:, :, :, 64:65], 1.0), writes=[RvS])

            with ExitStack() as s1:
                ctx = make_ffn_ctx(s1, "a_")
                u1T = sbt(s1, "u1T", [128, 8, 512], BF16)
                Ru1 = Res()
                u2T = sbt(s1, "u2T", [128, 8, 512], BF16)
                Ru2 = [Res() for _ in range(8)]
                xt = [sbt(s1, f"xt{i}", [128, 512], F32) for i in range(3)]
                Rxt = [Res() for _ in range(3)]
                x1t = [sbt(s1, f"x1t{i}", [128, 512], F32) for i in range(2)]
                Rx1t = [Res(), Res()]
                WV = sbt(s1, "WV", [128, 8, 512], BF16)
                RWV = Res()
                hval = sbt(s1, "hval", [128, 256], F32)
                Rhv = Res()
                S.dma("pool", WV[:], w_in_r[:, :, 1536:2048], writes=[RWV])
                S.dma("sp", hval[:], hval_in, writes=[Rhv])
                ntiles = 9 if stages >= 2 else 1
                xn = 0
                for ti in range(ntiles):
                    t0 = ti * 512
                    W = 512 if ti < 8 else 256
                    own = ti < 4
                    halo = ti == 8
                    for c in range(8):
                        xi = xn % 3
                        xn += 1
                        S.dma("sp", xt[xi][:, :W], xT[:, c, t0:t0 + W], writes=[Rxt[xi]])
                        S.op("act", lambda h: h.activation(out=u1T[:, c, :W], in_=xt[xi][:, :W], func=AF.Identity,
                                                           scale=vcol(V_A1, c), bias=vcol(V_SH1, c)),
                             reads=[Rxt[xi], Rc], writes=[Ru1])

                    def resid1(c):
                        S.dma("sp", ctx["z"][:, c, :W], xT[:, c, t0:t0 + W], writes=[ctx["Rz"][c]])

                    ffn(ctx, u1T, Ru1, W, f1g, f1u, f1d, V_C1, resid1)

                    def out_x1(c):
                        if not own:
                            return
                        i = c % 2
                        S.op("dve", lambda h: h.tensor_scalar(out=x1t[i][:, :W], in0=ctx["z"][:, c, :W],
                                                              scalar1=lng_t[:, c:c + 1], scalar2=lnb_t[:, c:c + 1],
                                                              op0=ALU.mult, op1=ALU.add),
                             reads=[ctx["Rz"][c], Rc], writes=[Rx1t[i]])
                        S.dma("sp", x1sp[:, c, t0:t0 + W], x1t[i][:, :W], reads=[Rx1t[i]])

                    def out_u2(c):
                        S.op("act", lambda h: h.activation(out=u2T[:, c, :W], in_=ctx["z"][:, c, :W], func=AF.Identity,
                                                           scale=vcol(V_G2, c), bias=vcol(V_B2, c)),
                             reads=[ctx["Rz"][c], Rc], writes=[Ru2[c]])

                    layernorm(ctx, W, [out_x1, out_u2])
                    if own:
                        S.dma("sp", u2sp[:, :, t0:t0 + W], u2T[:, :, :W], reads=Ru2)
                    if not halo:
                        for c in range(4):
                            wt, Rw = wa_load(ctx, w_in_r[:, :, 1024 + c * 128:1024 + (c + 1) * 128])
                            b = 4 + c % 2
                            for k in range(8):
                                S.op("pe", lambda h, k=k: h.matmul(banks[b][:, :W], wt[:, k, :], u2T[:, k, :W],
                                                                  start=(k == 0), stop=(k == 7)),
                                     reads=[Rw, Ru2[k]], writes=[Rb[b]])
                            S.op("act", lambda h: h.activation(out=kT[:, c, t0:t0 + W], in_=banks[b][:, :W], func=AF.Copy),
                                 reads=[Rb[b]], writes=[RkT])
                        i = ctx["wa_n"] % 8
                        ctx["wa_n"] += 1
                        wt, Rw = ctx["WA"][i], ctx["RWA"][i]
                        S.dma("pool", wt[:, :, 0:64], w_in_r[:, :, 2560:2624], writes=[Rw])
                        S.dma("pool", wt[:, :, 64:128], w_in_r[:, :, 2560:2624], writes=[Rw])
                        b = 4
                        for k in range(8):
                            S.op("pe", lambda h, k=k: h.matmul(banks[b][:, :W], wt[:, k, :], u2T[:, k, :W],
                                                              start=(k == 0), stop=(k == 7)),
                                 reads=[Rw, Ru2[k]], writes=[Rb[b]])
                        S.op("act", lambda h: h.activation(out=kiT[:, t0:t0 + W], in_=banks[b][:, :W], func=AF.Copy),
                             reads=[Rb[b]], writes=[RkiT])
                        for sbk in range(W // 128):
                            b = 2 + sbk % 2
                            for k in range(8):
                                S.op("pe", lambda h, k=k: h.matmul(banks[b][:, :], u2T[:, k, sbk * 128:(sbk + 1) * 128],
                                                                  WV[:, k, :], start=(k == 0), stop=(k == 7)),
                                     reads=[RWV, Ru2[k]], writes=[Rb[b]])
                            blk = t0 // 128 + sbk
                            S.op("dve", lambda h: h.tensor_copy(out=vS[:, blk, :, 0:64],
                                                                in_=banks[b][:, :].rearrange("p (h d) -> p h d", h=8)),
                                 reads=[Rb[b]], writes=[RvS])
                    else:
                        for g in range(4):
                            wt, Rw = wa_load(ctx, w_in_r[:, :, g * 128:(g + 1) * 128])
                            b = 4 + g % 2
                            for k in range(8):
                                S.op("pe", lambda h, k=k: h.matmul(banks[b][:, :W], wt[:, k, :], u2T[:, k, :W],
                                                                  start=(k == 0), stop=(k == 7)),
                                     reads=[Rw, Ru2[k]], writes=[Rb[b]])
                            S.op("dve", lambda h: h.tensor_tensor(out=phalo[:, g, :], in0=banks[b][:, :W], in1=hval[:],
                                                                  op=ALU.mult),
                                 reads=[Rb[b], Rhv], writes=[Rphalo])
                if debug:
                    S.dma("sp", dbgK, kT[:], reads=[RkT])
                    S.dma("sp", dbgV, vS[:].rearrange("p b h d -> p b (h d)"), reads=[RvS])
                S.barrier()

            if stages >= 3:
                with ExitStack() as s2:
                    stage2a(nc, S, s2, sbt, banks, ptb, Rb, Rpt, dict(
                        kT=kT, vS=vS, kiT=kiT, RkT=RkT, RvS=RvS, RkiT=RkiT, attn_sp=attn_sp, q_sp=q_sp, qi_sp=qi_sp,
                        u2sp=u2sp, w_in_r=w_in_r, sp_in=sp_in, im_in=im_in, b31_in=b31_in, ident=ident, Rc=Rc,
                        dbgS=(dbgS if debug else None), stages=stages))
                    S.barrier()
        S.barrier()

        if stages >= 5:
            with ExitStack() as s3:
                ctx = make_ffn_ctx(s3, "b_")
                u2t = sbt(s3, "u2t", [128, 8, 512], BF16)
                Ru2t = Res()
                WPL = sbt(s3, "WPL", [128, 4, 128], BF16)
                WAr = sbt(s3, "WAr", [128, 4, 1024], BF16)
                WBr = sbt(s3, "WBr", [128, 4, 1024], BF16)
                Rwres = Res()
                corr = sbt(s3, "corr", [128, 4, 128], F32)
                pb0 = sbt(s3, "pb0", [128, 4, 144], F32)
                pbA = sbt(s3, "pbA", [128, 4, 144], F32)
                pbB = sbt(s3, "pbB", [128, 4, 144], F32)
                Rpb = Res()
                pooled = sbt(s3, "pooled", [128, 512], BF16)
                Rpooled = Res()
                mixT = sbt(s3, "mixT", [128, 4, 512], BF16)
                Rmix = Res()
                mrgT = sbt(s3, "mrgT", [128, 8, 512], BF16)
                Rmrg = [Res() for _ in range(8)]
                sgt = [sbt(s3, f"sgt{i}", [128, 512], F32) for i in range(2)]
                Rsgt = [Res(), Res()]
                m1t = sbt(s3, "m1t", [128, 512], F32)
                Rm1 = Res()
                x2 = sbt(s3, "x2", [128, 8, 512], F32)
                Rx2 = [Res() for _ in range(8)]
                u3T = sbt(s3, "u3T", [128, 8, 512], BF16)
                Ru3 = Res()
                attnT = sbt(s3, "attnT", [128, 4, 512], BF16)
                RattnT = Res()
                ot = [sbt(s3, f"ot{i}", [128, 512], F32) for i in range(2)]
                Rot = [Res(), Res()]
                S.dma("pool", WPL[:], w_pool.rearrange("g c d -> c g d"), writes=[Rwres])
                S.dma("pool", WAr[:], w_a.rearrange("(k p) n -> p k n", p=128), writes=[Rwres])
                S.dma("pool", WBr[:], w_b.rearrange("(k p) n -> p k n", p=128), writes=[Rwres])
                S.dma("sp", corr[:], corr_in, writes=[Rwres])
                out_toks = []
                W = 512
                for it in range(4 if stages >= 6 else 1):
                    t0 = it * 512
                    S.dma("sp", u2t[:], u2sp[:, :, t0:t0 + W], writes=[Ru2t])
                    S.dma("sp", attnT[:], attn_sp[:, :, t0:t0 + W], writes=[RattnT])
                    for g in range(4):
                        wt, Rw = wa_load(ctx, w_in_r[:, :, g * 128:(g + 1) * 128])
                        b = 4 + g % 2
                        for k in range(8):
                            S.op("pe", lambda h, k=k: h.matmul(banks[b][:, :W], wt[:, k, :], u2t[:, k, :],
                                                              start=(k == 0), stop=(k == 7)),
                                 reads=[Rw, Ru2t], writes=[Rb[b]])
                        S.op("dve", lambda h: h.tensor_copy(out=pb0[:, :, 0:16],
                                                            in_=phalo[:, g, it * 64:(it + 1) * 64].rearrange("p (b t) -> p b t", b=4)),
                             reads=[Rphalo], writes=[Rpb])
                        S.op("dve", lambda h: h.tensor_copy(out=pb0[:, :, 16:144],
                                                            in_=banks[b][:, :].rearrange("p (b t) -> p b t", b=4)),
                             reads=[Rb[b]], writes=[Rpb])
                        src = pb0
                        dsts = [pbA, pbB]
                        for i in range(g + 1):
                            sh = 1 << i
                            dst = dsts[i % 2]
                            S.op("dve", lambda h, src=src, dst=dst, sh=sh: h.tensor_tensor(
                                out=dst[:, :, sh:144], in0=src[:, :, sh:144], in1=src[:, :, 0:144 - sh], op=ALU.add),
                                reads=[Rpb], writes=[Rpb])
                            src = dst
                        wwin = float(1 << (g + 1))
                        S.op("dve", lambda h, src=src: h.tensor_scalar_mul(out=src[:, :, 16:144], in0=src[:, :, 16:144],
                                                                         scalar1=1.0 / wwin), reads=[Rpb], writes=[Rpb])
                        if it == 0:
                            S.op("dve", lambda h, src=src: h.tensor_tensor(out=src[:, 0, 16:144], in0=src[:, 0, 16:144],
                                                                         in1=corr[:, g, :], op=ALU.mult),
                                 reads=[Rpb, Rwres], writes=[Rpb])
                        S.op("dve", lambda h, src=src: h.tensor_tensor(out=pooled[:].rearrange("p (b t) -> p b t", b=4),
                                                                     in0=src[:, :, 16:144], in1=pb0[:, :, 16:144],
                                                                     op=ALU.subtract),
                             reads=[Rpb], writes=[Rpooled])
                        b2 = 2 + g % 2
                        S.op("pe", lambda h: h.matmul(banks[b2][:, :W], WPL[:, g, :], pooled[:], start=True, stop=True),
                             reads=[Rwres, Rpooled], writes=[Rb[b2]])
                        S.op("act", lambda h: h.activation(out=mixT[:, g, :], in_=banks[b2][:, :W], func=AF.Copy,
                                                           scale=psc_t[:, g:g + 1]),
                             reads=[Rb[b2], Rc], writes=[Rmix])
                    for c in range(8):
                        for br in range(2):
                            wres = WAr if br == 0 else WBr
                            rhs_t = mixT if br == 0 else attnT
                            gcol = 2632 + br * 1024 + c * 128
                            wt, Rw = wa_load(ctx, w_in_r[:, :, gcol:gcol + 128])
                            by = 4 + br
                            bgt = 2 + br
                            for k in range(4):
                                rhs = rhs_t[:, k, :]
                                S.op("pe", lambda h, k=k, rhs=rhs: h.matmul(banks[by][:, :W], wres[:, k, c * 128:(c + 1) * 128], rhs,
                                                                            start=(k == 0), stop=(k == 3)),
                                     reads=[Rwres, Rmix, RattnT], writes=[Rb[by]])
                            for k in range(8):
                                S.op("pe", lambda h, k=k: h.matmul(banks[bgt][:, :W], wt[:, k, :], u2t[:, k, :],
                                                                  start=(k == 0), stop=(k == 7)),
                                     reads=[Rw, Ru2t], writes=[Rb[bgt]])
                            S.op("act", lambda h: h.activation(out=sgt[br][:], in_=banks[bgt][:, :W], func=AF.Sigmoid),
                                 reads=[Rb[bgt]], writes=[Rsgt[br]])
                            if br == 0:
                                S.op("dve", lambda h: h.tensor_tensor(out=m1t[:], in0=banks[by][:, :W], in1=sgt[0][:], op=ALU.mult),
                                     reads=[Rb[by], Rsgt[0]], writes=[Rm1])
                            else:
                                S.op("dve", lambda h: h.tensor_tensor(out=sgt[1][:], in0=banks[by][:, :W], in1=sgt[1][:], op=ALU.mult),
                                     reads=[Rb[by], Rsgt[1]], writes=[Rsgt[1]])
                                S.op("dve", lambda h: h.tensor_tensor(out=mrgT[:, c, :], in0=m1t[:], in1=sgt[1][:], op=ALU.add),
                                     reads=[Rm1, Rsgt[1]], writes=[Rmrg[c]])
                    for c in range(8):
                        wt, Rw = wa_load(ctx, w_out_r[:, :, c * 128:(c + 1) * 128])
                        S.dma("sp", ctx["z"][:, c, :], x1sp[:, c, t0:t0 + W], writes=[ctx["Rz"][c]])
                        by = 4 + c % 2
                        for k in range(8):
                            S.op("pe", lambda h, k=k: h.matmul(banks[by][:, :W], wt[:, k, :], mrgT[:, k, :],
                                                              start=(k == 0), stop=(k == 7)),
                                 reads=[Rw, Rmrg[k]], writes=[Rb[by]])
                        S.op("dve", lambda h: h.scalar_tensor_tensor(out=ctx["z"][:, c, :], in0=banks[by][:, :W],
                                                                     scalar=vcol(V_C2, c), in1=ctx["z"][:, c, :],
                                                                     op0=ALU.mult, op1=ALU.add),
                             reads=[Rb[by], ctx["Rz"][c], Rc], writes=[ctx["Rz"][c]])

                    def out_x2(c):
                        S.op("dve", lambda h: h.tensor_scalar(out=x2[:, c, :], in0=ctx["z"][:, c, :],
                                                              scalar1=lng_t[:, 8 + c:9 + c], scalar2=lnb_t[:, 8 + c:9 + c],
                                                              op0=ALU.mult, op1=ALU.add),
                             reads=[ctx["Rz"][c], Rc], writes=[Rx2[c]])

                    def out_u3(c):
                        S.op("act", lambda h: h.activation(out=u3T[:, c, :], in_=ctx["z"][:, c, :], func=AF.Identity,
                                                           scale=vcol(V_G3, c), bias=vcol(V_B3, c)),
                             reads=[ctx["Rz"][c], Rc], writes=[Ru3])

                    layernorm(ctx, W, [out_x2, out_u3])

                    def resid3(c):
                        S.op("act", lambda h: h.activation(out=ctx["z"][:, c, :], in_=x2[:, c, :], func=AF.Copy),
                             reads=[Rx2[c]], writes=[ctx["Rz"][c]])

                    ffn(ctx, u3T, Ru3, W, f2g, f2u, f2d, V_C3, resid3)

                    def out_fin(c):
                        i = c % 2
                        S.op("dve", lambda h: h.tensor_scalar(out=ot[i][:], in0=ctx["z"][:, c, :],
                                                              scalar1=lng_t[:, 16 + c:17 + c], scalar2=lnb_t[:, 16 + c:17 + c],
                                                              op0=ALU.mult, op1=ALU.add),
                             reads=[ctx["Rz"][c], Rc], writes=[Rot[i]])
                        out_toks.append(S.dma("sp", outT[:, c, t0:t0 + W], ot[i][:], reads=[Rot[i]]))

                    layernorm(ctx, W, [out_fin])
                S.barrier()
        S.barrier()
        for q in S.dtoks.values():
            for t in q:
                S._wait("sp", t)
    return nc


def stage2a(nc, S, s2, sbt, banks, ptb, Rb, Rpt, P):
    kT, vS, kiT = P["kT"], P["vS"], P["kiT"]
    RkT, RvS, RkiT = P["RkT"], P["RvS"], P["RkiT"]
    attn_sp = P["attn_sp"]
    w_in_r = P["w_in_r"]
    ident, Rc = P["ident"], P["Rc"]
    stages = P["stages"]
    SP = sbt(s2, "SP", [128, 8, 3, 128], F32)
    IM = sbt(s2, "IM", [128, 2, 128], F32)
    b31 = sbt(s2, "b31", [128, 8], F32)
    Rk = Res()
    S.dma("sp", SP[:], P["sp_in"], writes=[Rk])
    S.dma("sp", IM[:], P["im_in"], writes=[Rk])
    S.dma("sp", b31[:], P["b31_in"], writes=[Rk])
    q_sp, qi_sp = P["q_sp"], P["qi_sp"]
    qblk = [sbt(s2, f"qblk{i}", [128, 4, 128], BF16) for i in range(4)]
    Rqblk = [Res() for _ in range(4)]
    qiblk = [sbt(s2, f"qiblk{i}", [128, 4, 128], BF16) for i in range(3)]
    Rqiblk = [Res() for _ in range(3)]
    wtm = sbt(s2, "wtm", [128, 16, 8], F32)
    Rwtm = Res()
    cnt = {"bk": 0, "bi": 0, "bl": 0, "ar": 0, "pt": 0, "tb": 0, "qs": 0}

    def bank():
        rot = (0, 1, 2, 3)
        b = rot[cnt["bk"] % len(rot)]
        cnt["bk"] += 1
        return b

    def bank_idx():
        b = (3, 6)[cnt["bi"] % 2]
        cnt["bi"] += 1
        return b

    def bank_log():
        b = (0, 1, 2)[cnt["bl"] % 3]
        cnt["bl"] += 1
        return b

    nblocks = 16 if stages >= 4 else 1
    with ExitStack() as sq:
        WQ = [sbt(sq, f"WQ{i}", [128, 8, 128], BF16) for i in range(8)]
        RWQ = Res()
        WWI = sbt(sq, "WWI", [128, 8, 8], BF16)
        for c in range(4):
            S.dma("pool", WQ[c][:], w_in_r[:, :, 512 + c * 128:512 + (c + 1) * 128], writes=[RWQ])
            S.dma("pool", WQ[4 + c][:], w_in_r[:, :, 2048 + c * 128:2048 + (c + 1) * 128], writes=[RWQ])
        S.dma("pool", WWI[:], w_in_r[:, :, 2624:2632], writes=[RWQ])
        u2t = [sbt(sq, f"u2q{i}", [128, 8, 512], BF16) for i in range(2)]
        Ru2t = [Res(), Res()]
        qst = [sbt(sq, f"qst{i}", [128, 512], BF16) for i in range(3)]
        Rqst = [Res() for _ in range(3)]
        for it in range((nblocks + 3) // 4):
            t0 = it * 512
            ut, Ru = u2t[it % 2], Ru2t[it % 2]
            S.dma("sp", ut[:], P["u2sp"][:, :, t0:t0 + 512], writes=[Ru])
            for wi_, dsp in enumerate((q_sp, qi_sp)):
                for c in range(4):
                    b = bank()
                    for k in range(8):
                        S.op("pe", lambda h, k=k: h.matmul(banks[b][:, :], WQ[wi_ * 4 + c][:, k, :], ut[:, k, :],
                                                          start=(k == 0), stop=(k == 7)),
                             reads=[RWQ, Ru], writes=[Rb[b]])
                    qi_ = cnt["qs"] % 3
                    cnt["qs"] += 1
                    S.op("act", lambda h: h.activation(out=qst[qi_][:], in_=banks[b][:, :], func=AF.Copy),
                         reads=[Rb[b]], writes=[Rqst[qi_]])
                    S.dma("sp", dsp[:, c, t0:t0 + 512], qst[qi_][:], reads=[Rqst[qi_]])
            for jj in range(4):
                j = it * 4 + jj
                for k in range(8):
                    S.op("pe", lambda h, k=k: h.matmul(banks[6][:, jj * 8:jj * 8 + 8], ut[:, k, jj * 128:(jj + 1) * 128],
                                                      WWI[:, k, :], start=(k == 0), stop=(k == 7)),
                         reads=[RWQ, Ru], writes=[Rb[6]])
            S.op("dve", lambda h: h.tensor_copy(out=wtm[:, it * 4:it * 4 + 4, :],
                                                in_=banks[6][:, 0:32].rearrange("p (a b) -> p a b", a=4)),
                 reads=[Rb[6]], writes=[Rwtm])
        S.barrier()

    Sc = [sbt(s2, f"Sc{i}", [128, 4096], F32) for i in range(3)]
    RSc = [Res() for _ in range(3)]
    msk = [sbt(s2, f"msk{i}", [128, 4096], BF16) for i in range(3)]
    Rmsk = [Res() for _ in range(3)]
    mskT = [sbt(s2, f"mskT{i}", [128, 32, 128], BF16) for i in range(2)]
    RmskT = [Res(), Res()]
    ar = [sbt(s2, f"ar{i}", [128, 512], F32) for i in range(3)]
    Rar = [Res() for _ in range(3)]
    pt_ = [sbt(s2, f"pt{i}", [128, 512], BF16) for i in range(5)]
    Rpt_ = [Res() for _ in range(5)]
    tmpb = [sbt(s2, f"tmpb{i}", [128, 128], F32) for i in range(2)]
    Rtmpb = [Res(), Res()]
    smt = [sbt(s2, f"sm{i}", [128, 8], F32) for i in range(3)]
    Rsm = [Res() for _ in range(3)]
    rec = sbt(s2, "rec", [128, 8], F32)
    Rrec = Res()
    abf = sbt(s2, "abf", [128, 8, 64], BF16)
    Rabf = Res()
    atile = [sbt(s2, f"atile{i}", [128, 4, 128], BF16) for i in range(2)]
    Ratile = [Res(), Res()]
    LO = -1.0 / 1024.0
    RNG = 1.0 + 2.0 / 1024.0

    def gen_A(j):
        s = j % 3
        nk = j + 1
        sc, Rs = Sc[s], RSc[s]
        sm, Rm = smt[s], Rsm[s]
        qib, Rqib = qiblk[j % 3], Rqiblk[j % 3]
        S.dma("sp", qib[:], qi_sp[:, :, j * 128:(j + 1) * 128], writes=[Rqib])
        S.dma("sp", qblk[j % 4][:], q_sp[:, :, j * 128:(j + 1) * 128], writes=[Rqblk[j % 4]])
        for base in (0, 2048):
            for cs in range(0, nk * 128, 512):
                wd = min(512, nk * 128 - cs)
                for hh in range(8):
                    po = 64 * (hh % 2)
                    b = bank_idx()
                    S.op("pe", lambda h: h.matmul(banks[b][:, :wd], qib[po:po + 64, hh // 2, :],
                                                  kiT[po:po + 64, base + cs:base + cs + wd], start=True, stop=True),
                         reads=[Rqib, RkiT], writes=[Rb[b]])
                    ai = cnt["ar"] % 3
                    cnt["ar"] += 1
                    S.op("act", lambda h: h.activation(out=ar[ai][:, :wd], in_=banks[b][:, :wd], func=AF.Relu),
                         reads=[Rb[b]], writes=[Rar[ai]])
                    dst = sc[:, base + cs:base + cs + wd]
                    if hh == 0:
                        S.op("dve", lambda h: h.tensor_scalar_mul(out=dst, in0=ar[ai][:, :wd], scalar1=wtm[:, j, 0:1]),
                             reads=[Rar[ai], Rwtm], writes=[Rs])
                    else:
                        S.op("dve", lambda h: h.scalar_tensor_tensor(out=dst, in0=ar[ai][:, :wd], scalar=wtm[:, j, hh:hh + 1],
                                                                      in1=dst, op0=ALU.mult, op1=ALU.add),
                             reads=[Rar[ai], Rwtm, Rs], writes=[Rs])
                    yield
        Sv = sc[:].rearrange("p (a s) -> p a s", a=2)[:, :, 0:nk * 128]
        Mv = msk[s][:].rearrange("p (a s) -> p a s", a=2)[:, :, 0:nk * 128]
        S.op("dve", lambda h: h.tensor_reduce(out=sm[:, 0:1], in_=Sv, axis=AX.XY, op=ALU.max), reads=[Rs], writes=[Rm])
        S.op("dve", lambda h: h.tensor_reduce(out=sm[:, 1:2], in_=Sv, axis=AX.XY, op=ALU.min), reads=[Rs], writes=[Rm])
        yield
        S.op("dve", lambda h: h.scalar_tensor_tensor(out=sm[:, 2:3], in0=sm[:, 0:1], scalar=1e-20, in1=sm[:, 1:2],
                                                     op0=ALU.add, op1=ALU.subtract), reads=[Rm], writes=[Rm])
        S.op("dve", lambda h: h.reciprocal(out=sm[:, 2:3], in_=sm[:, 2:3]), reads=[Rm], writes=[Rm])
        S.op("dve", lambda h: h.scalar_tensor_tensor(out=sm[:, 3:4], in0=sm[:, 1:2], scalar=-1.0, in1=sm[:, 2:3],
                                                     op0=ALU.mult, op1=ALU.mult), reads=[Rm], writes=[Rm])
        S.op("act", lambda h: h.activation(out=Sv, in_=Sv, func=AF.Identity, scale=sm[:, 2:3], bias=sm[:, 3:4]),
             reads=[Rs, Rm], writes=[Rs])
        for mi, base in ((0, 0), (1, 2048)):
            dst = sc[:, base + j * 128:base + (j + 1) * 128]
            S.op("dve", lambda h: h.tensor_tensor(out=dst, in0=dst, in1=IM[:, mi, :], op=ALU.add),
                 reads=[Rs, Rk], writes=[Rs])
        yield
        step = RNG / 2
        if j % 2 == 1:
            n_tot = 2 * nk * 128
            S.op("dve", lambda h: h.memset(sm[:, 4:5], -(LO + step)), reads=[Rm], writes=[Rm])
            cur = 4
            for i in range(NBIS):
                S.op("act", lambda h: h.activation(out=Mv, in_=Sv, func=AF.Sign, bias=sm[:, cur:cur + 1],
                                                   accum_out=sm[:, 6:7]),
                     reads=[Rs, Rm], writes=[Rm, Rmsk[s]])
                S.op("act", lambda h: h.activation(out=sm[:, 7:8], in_=sm[:, 6:7], func=AF.Sign, bias=float(n_tot - 511)),
                     reads=[Rm], writes=[Rm])
                nxt = 9 - cur
                sc_ = -(step / 2)
                S.op("act", lambda h: h.activation(out=sm[:, nxt:nxt + 1], in_=sm[:, 7:8], func=AF.Identity, scale=sc_,
                                                   bias=sm[:, cur:cur + 1]), reads=[Rm], writes=[Rm])
                cur = nxt
                if i < NBIS - 1:
                    step = step / 2
                yield
            S.op("dve", lambda h: h.tensor_scalar(out=Mv, in0=Sv, scalar1=sm[:, cur:cur + 1], scalar2=-(step / 2),
                                                  op0=ALU.add, op1=ALU.is_ge),
                 reads=[Rs, Rm], writes=[Rmsk[s]])
            yield
            return
        S.op("dve", lambda h: h.memset(sm[:, 4:5], LO + step), reads=[Rm], writes=[Rm])
        for i in range(NBIS):
            S.op("dve", lambda h: h.tensor_scalar(out=Mv, in0=Sv, scalar1=sm[:, 4:5], scalar2=0.0,
                                                  op0=ALU.is_ge, op1=ALU.add, accum_out=sm[:, 5:6]),
                 reads=[Rs, Rm], writes=[Rm, Rmsk[s]])
            if i < NBIS - 1:
                nstep = step / 2
                S.op("dve", lambda h: h.tensor_scalar(out=sm[:, 6:7], in0=sm[:, 5:6], scalar1=255.5, scalar2=2.0 * nstep,
                                                      op0=ALU.is_ge, op1=ALU.mult), reads=[Rm], writes=[Rm])
                S.op("dve", lambda h: h.scalar_tensor_tensor(out=sm[:, 4:5], in0=sm[:, 6:7], scalar=-nstep, in1=sm[:, 4:5],
                                                             op0=ALU.add, op1=ALU.add), reads=[Rm], writes=[Rm])
                step = nstep
            else:
                S.op("dve", lambda h: h.tensor_scalar(out=sm[:, 6:7], in0=sm[:, 5:6], scalar1=255.5, scalar2=step,
                                                      op0=ALU.is_lt, op1=ALU.mult), reads=[Rm], writes=[Rm])
                S.op("dve", lambda h: h.tensor_tensor(out=sm[:, 4:5], in0=sm[:, 4:5], in1=sm[:, 6:7], op=ALU.subtract),
                     reads=[Rm], writes=[Rm])
            yield
        S.op("dve", lambda h: h.tensor_scalar(out=Mv, in0=Sv, scalar1=sm[:, 4:5], scalar2=None, op0=ALU.is_ge),
             reads=[Rs, Rm], writes=[Rmsk[s]])
        if P["dbgS"] is not None and j == 3:
            S.dma("sp", P["dbgS"], sc[:], reads=[Rs])
        yield

    def gen_B(j):
        s = j % 3
        nk = j + 1
        mT, RmT = mskT[j % 2], RmskT[j % 2]
        qb, Rqb = qblk[j % 4], Rqblk[j % 4]
        kbs = list(range(0, nk)) + list(range(16, 16 + nk))
        for g0 in range(0, len(kbs), 8):
            grp = kbs[g0:g0 + 8]
            for i, kb in enumerate(grp):
                S.op("pe", lambda h, i=i, kb=kb: h.transpose(ptb[:, i * 128:(i + 1) * 128], msk[s][:, kb * 128:(kb + 1) * 128], ident[:]),
                     reads=[Rmsk[s], Rc], writes=[Rpt])
            runs = []
            for i, kb in enumerate(grp):
                if runs and runs[-1][1] + runs[-1][2] == kb:
                    runs[-1][2] += 1
                else:
                    runs.append([i, kb, 1])
            for (i0, kb0, n) in runs:
                S.op("act", lambda h, i0=i0, kb0=kb0, n=n: h.activation(
                    out=mT[:, kb0:kb0 + n, :], in_=ptb[:, i0 * 128:(i0 + n) * 128].rearrange("p (a t) -> p a t", a=n),
                    func=AF.Copy), reads=[Rpt], writes=[RmT])
            yield
        far = []
        for lo_, hi_ in ((0, j), (16, 16 + j - 1)):
            x = lo_
            while x < hi_:
                n = min(4, hi_ - x)
                far.append((x, n))
                x += n
        specials = [(j, 1), (16 + j, 2)]
        if j >= 1:
            specials.append((16 + j - 1, 0))
        units = []
        for hh in range(8):
            ul = [("far", kb0, n, None) for kb0, n in far] + [("sp", kb, 1, si) for kb, si in specials]
            nun = sum(u[2] for u in ul)
            done = 0
            for kind, kb0, n, si in ul:
                units.append((hh, kind, kb0, n, si, done, nun))
                done += n
        st = {}

        def emit_L(u):
            hh, kind, kb0, n, si, done, nun = units[u]
            po = 64 * (hh % 2)
            b = bank_log()
            for i in range(n):
                kb = kb0 + i
                S.op("pe", lambda h, i=i, kb=kb: h.matmul(banks[b][:, i * 128:(i + 1) * 128],
                                                        kT[po:po + 64, hh // 2, kb * 128:(kb + 1) * 128],
                                                        qb[po:po + 64, hh // 2, :], start=True, stop=True),
                     reads=[RkT, Rqb], writes=[Rb[b]])
            st[u] = b

        def emit_post(u):
            hh, kind, kb0, n, si, done, nun = units[u]
            b = st[u]
            pi = cnt["pt"] % 5
            cnt["pt"] += 1
            pt, Rp = pt_[pi], Rpt_[pi]
            if kind == "far":
                S.op("act", lambda h: h.activation(out=pt[:, :n * 128], in_=banks[b][:, :n * 128], func=AF.Exp,
                                                   scale=0.125, bias=b31[:, hh:hh + 1]),
                     reads=[Rb[b], Rk], writes=[Rp])
            else:
                ti = cnt["tb"] % 2
                cnt["tb"] += 1
                S.op("dve", lambda h: h.scalar_tensor_tensor(out=tmpb[ti][:], in0=banks[b][:, 0:128], scalar=0.125,
                                                             in1=SP[:, hh, si, :], op0=ALU.mult, op1=ALU.add),
                     reads=[Rb[b], Rk], writes=[Rtmpb[ti]])
                S.op("act", lambda h: h.activation(out=pt[:, :128], in_=tmpb[ti][:], func=AF.Exp),
                     reads=[Rtmpb[ti]], writes=[Rp])
            S.op("pool", lambda h: h.tensor_tensor(out=pt[:, :n * 128], in0=pt[:, :n * 128],
                                                   in1=mT[:, kb0:kb0 + n, :].rearrange("p a t -> p (a t)"), op=ALU.mult),
                 reads=[Rp, RmT], writes=[Rp])
            st[u] = (pt, Rp)

        def emit_PV(u):
            hh, kind, kb0, n, si, done, nun = units[u]
            pt, Rp = st[u]
            ob = 4 + hh // 4
            oc = (hh % 4) * 65
            for i in range(n):
                kb = kb0 + i
                S.op("pe", lambda h, i=i, kb=kb: h.matmul(
                    banks[ob][:, oc:oc + 65], pt[:, i * 128:(i + 1) * 128], vS[:, kb, hh, :],
                    start=(done + i == 0), stop=(done + i == nun - 1)),
                    reads=[Rp, RvS], writes=[Rb[ob]])

        NU = len(units)
        LA = 2
        for u in range(min(LA, NU)):
            emit_L(u)
        LAG = 2
        for u in range(NU + LAG):
            if u < NU:
                emit_post(u)
                if u + LA < NU:
                    emit_L(u + LA)
            if u - LAG >= 0:
                emit_PV(u - LAG)
            yield
        for half in range(2):
            ov = banks[4 + half][:, 0:260].rearrange("p (h d) -> p h d", h=4)
            S.op("dve", lambda h: h.reciprocal(out=rec[:, half * 4:half * 4 + 4], in_=ov[:, :, 64]),
                 reads=[Rb[4 + half]], writes=[Rrec])
            for h4 in range(4):
                hh = half * 4 + h4
                S.op("dve", lambda h, hh=hh, h4=h4: h.tensor_scalar_mul(out=abf[:, hh, :], in0=ov[:, h4, 0:64],
                                                                      scalar1=rec[:, hh:hh + 1]),
                     reads=[Rb[4 + half], Rrec], writes=[Rabf])
        for c in range(4):
            S.op("pe", lambda h, c=c: h.transpose(ptb[:, c * 128:(c + 1) * 128],
                                                  abf[:, 2 * c:2 * c + 2, :].rearrange("p h d -> p (h d)"), ident[:]),
                 reads=[Rabf, Rc], writes=[Rpt])
        S.op("act", lambda h: h.activation(out=atile[j % 2][:], in_=ptb[:, 0:512].rearrange("p (c t) -> p c t", c=4), func=AF.Copy),
             reads=[Rpt], writes=[Ratile[j % 2]])
        S.dma("sp", attn_sp[:, :, j * 128:(j + 1) * 128], atile[j % 2][:], reads=[Ratile[j % 2]])
        yield

    def n_yields_A(j):
        nk = j + 1
        per_range = sum(8 for _ in range(0, nk * 128, 512))
        return 2 * per_range + 2 + NBIS + 1

    gens = {}
    prog = {}

    def adv(j, n):
        if j >= nblocks:
            return
        if j not in gens:
            gens[j] = gen_A(j)
            prog[j] = 0
        for _ in range(n):
            try:
                next(gens[j])
                prog[j] += 1
            except StopIteration:
                prog[j] = 10 ** 9
                return

    adv(0, 10 ** 6)
    adv(1, n_yields_A(1) // 2)
    for j in range(nblocks):
        gb = gen_B(j)
        nk = j + 1
        nB = (2 * nk + 7) // 8 + 8 * (len(range(0, j, 4)) + len(range(0, max(j - 1, 0), 4)) + (3 if j >= 1 else 2)) + 3
        rem1 = max(n_yields_A(j + 1) + 1 - prog.get(j + 1, 0), 0) if j + 1 < nblocks else 0
        q2 = n_yields_A(j + 2) // 2 if j + 2 < nblocks else 0
        a1 = a2 = 0.0
        b_done = False
        while not b_done:
            try:
                next(gb)
            except StopIteration:
                b_done = True
            a1 += rem1 / nB
            a2 += q2 / nB
            while a1 >= 1.0:
                adv(j + 1, 1)
                a1 -= 1.0
            while a2 >= 1.0:
                adv(j + 2, 1)
                a2 -= 1.0
        adv(j + 1, 10 ** 6)


def _t5_bucket(n):
    n = np.asarray(n)
    nf = np.maximum(n, 1).astype(np.float32)
    large = 16 + (np.log(nf / np.float32(16)) / np.float32(math.log(128 / 16)) * np.float32(16)).astype(np.int32)
    large = np.minimum(large, 31)
    return np.where(n < 16, n, large)


def _fm(v):
    v = np.asarray(v, np.float32).reshape(-1, 8, 128)
    return np.ascontiguousarray(v.transpose(2, 0, 1).reshape(128, -1))


_NC_CACHE = {}


def prep_core(core, x, c, b_ada, ln_g, ln_b, pool_scale, rel_bias):
    b, r = core // 2, core % 2
    own_blocks = [2 * j + r for j in range(16)]
    oth_blocks = [2 * j + 1 - r for j in range(16)]
    idx = []
    for g in own_blocks:
        idx += list(range(g * 128, (g + 1) * 128))
    for g in oth_blocks:
        idx += list(range(g * 128, (g + 1) * 128))
    hv = np.ones(256, np.float32)
    for j, g in enumerate(own_blocks):
        for i in range(16):
            t = g * 128 - 16 + i
            if t < 0:
                hv[j * 16 + i] = 0.0
                t = 0
            idx.append(t)
    idx = np.asarray(idx)
    xs = x[b][idx]
    xTl = np.ascontiguousarray(xs.T.reshape(8, 128, NTOK).transpose(1, 0, 2))
    s_i = np.arange(128)[:, None]
    t_i = np.arange(128)[None, :]
    sp = np.zeros((128, 8, 3, 128), np.float32)
    d_own = np.maximum(t_i - s_i, 0)
    sp[:, :, 1, :] = rel_bias[_t5_bucket(d_own)].transpose(0, 2, 1)
    d_near = 128 + t_i - s_i
    near = rel_bias[_t5_bucket(d_near)].transpose(0, 2, 1)
    far = np.broadcast_to(rel_bias[31][None, :, None], (128, 8, 128))
    if r == 1:
        sp[:, :, 2, :] = near
        sp[:, :, 0, :] = far
    else:
        sp[:, :, 2, :] = 0.0
        sp[:, :, 0, :] = near
    im = np.zeros((128, 2, 128), np.float32)
    tt = np.arange(128)[:, None]
    ss = np.arange(128)[None, :]
    im[:, 0, :] = np.where(ss <= tt, 0.0, NEG)
    im[:, 1, :] = 0.0 if r == 1 else NEG
    b31 = np.ascontiguousarray(np.broadcast_to(rel_bias[31][None, :], (128, 8))).astype(np.float32)
    corr = np.ones((128, 4, 128), np.float32)
    if r == 0:
        for g in range(4):
            w = 1 << (g + 1)
            tpos = np.arange(128)
            corr[:, g, :] = (w / np.minimum(tpos + 1, w)).astype(np.float32)[None, :]
    return {
        "xT": xTl,
        "c_fm": _fm(c[b]),
        "bada": _fm(b_ada[0]),
        "lng": _fm(ln_g[0].reshape(-1)),
        "lnb": _fm(ln_b[0].reshape(-1)),
        "psc": np.ascontiguousarray(pool_scale[0].reshape(4, 128).T),
        "sp_in": sp, "im_in": im, "b31_in": b31,
        "hval_in": np.ascontiguousarray(np.broadcast_to(hv[None, :], (128, 256))),
        "corr_in": corr,
    }


def kernel(x, c, w_ada, b_ada, ln_g, ln_b, ffn1_w_gate, ffn1_w_up, ffn1_w_down,
           w_in, w_pool, pool_scale, w_a, w_b, w_out, rel_bias,
           ffn2_w_gate, ffn2_w_up, ffn2_w_down):
    A = lambda a: np.ascontiguousarray(np.asarray(a, dtype=np.float32))
    x, c, b_ada, ln_g, ln_b, pool_scale, rel_bias = map(A, (x, c, b_ada, ln_g, ln_b, pool_scale, rel_bias))
    shared = {
        "w_ada": A(w_ada)[0], "f1g": A(ffn1_w_gate)[0], "f1u": A(ffn1_w_up)[0], "f1d": A(ffn1_w_down)[0],
        "w_in": A(w_in)[0], "w_pool": A(w_pool)[0], "w_a": A(w_a)[0], "w_b": A(w_b)[0], "w_out": A(w_out)[0],
        "f2g": A(ffn2_w_gate)[0], "f2u": A(ffn2_w_up)[0], "f2d": A(ffn2_w_down)[0],
    }
    if "nc" not in _NC_CACHE:
        _NC_CACHE["nc"] = build(debug=False, stages=9)
    nc = _NC_CACHE["nc"]
    in_maps = []
    for core in range(8):
        m = dict(shared)
        m.update(prep_core(core, x, c, b_ada, ln_g, ln_b, pool_scale, rel_bias))
        in_maps.append(m)
    res = run_bass_kernel_spmd(nc, in_maps, core_ids=list(range(8)))
    out = np.zeros((4, S_LEN, D), np.float32)
    for core in range(8):
        b, r = core // 2, core % 2
        o = res.results[core]["outT"]
        o = o.transpose(2, 1, 0).reshape(NOWN, D)
        for j in range(16):
            g = 2 * j + r
            out[b, g * 128:(g + 1) * 128] = o[j * 128:(j + 1) * 128]
    return out
```

```python
import math
import os
import numpy as np
import concourse.bass as bass
import concourse.mybir as mybir
from concourse.bass_utils import run_bass_kernel_spmd
from contextlib import ExitStack

F32 = mybir.dt.float32
BF16 = mybir.dt.bfloat16
AF = mybir.ActivationFunctionType
ALU = mybir.AluOpType
AX = mybir.AxisListType

D = 1024
S_LEN = 4096
NOWN = 2048
NTOK = 4352
DFF = 2816
NF = 22
INC = 4680
ALPHA = 2.0 ** 0.25
EPS_P = 1e-5 / (ALPHA * ALPHA)
NEG = -30000.0
XR = 4096.0
NBIS = 22
NDS = 32
STAGES = 9
DEBUG = False


class Res:
    __slots__ = ("w", "r")

    def __init__(self):
        self.w = None
        self.r = []


class Sched:
    def __init__(self, nc, es):
        self.nc = nc
        self.E = {}
        for name, h in [("pe", nc.tensor), ("act", nc.scalar), ("dve", nc.vector),
                        ("pool", nc.gpsimd), ("sp", nc.sync)]:
            sem = es.enter_context(nc.semaphore("s_" + name))
            self.E[name] = dict(h=h, sem=sem, n=0, seen={})
        self.dsems = {q: [es.enter_context(nc.semaphore(f"dq{q}{i}")) for i in range(NDS)] for q in ("sp", "pool")}
        self.dn = {"sp": 0, "pool": 0}
        self.dtoks = {"sp": [None] * NDS, "pool": [None] * NDS}

    def _wait(self, e, tok):
        if tok is None:
            return
        sem, val, owner = tok
        E = self.E[e]
        if owner == e and e in ("pe", "sp"):
            return
        key = sem.num
        if E["seen"].get(key, 0) >= val:
            return
        E["h"].wait_ge(sem, val)
        E["seen"][key] = val

    def _deps(self, e, reads, writes):
        for r in reads:
            self._wait(e, r.w)
        for w in writes:
            self._wait(e, w.w)
            for t in w.r:
                self._wait(e, t)

    def _mark(self, tok, reads, writes):
        for r in reads:
            r.r.append(tok)
            if len(r.r) > 48:
                best = {}
                for t in r.r:
                    k = t[0].num
                    if k not in best or best[k][1] < t[1]:
                        best[k] = t
                r.r = list(best.values())
        for w in writes:
            w.w = tok
            w.r = []

    def op(self, e, fn, reads=(), writes=()):
        self._deps(e, reads, writes)
        E = self.E[e]
        ins = fn(E["h"])
        E["n"] += 1
        ins.then_inc(E["sem"], 1)
        tok = (E["sem"], E["n"], e)
        self._mark(tok, reads, writes)
        return tok

    def dma(self, e, out, in_, reads=(), writes=()):
        i = self.dn[e] % NDS
        self.dn[e] += 1
        self._wait(e, self.dtoks[e][i])
        self._deps(e, reads, writes)
        E = self.E[e]
        ins = E["h"].dma_start(out=out, in_=in_)
        val = 16 * ((self.dn[e] - 1) // NDS + 1)
        ins.then_inc(self.dsems[e][i], 16)
        tok = (self.dsems[e][i], val, None)
        self.dtoks[e][i] = tok
        self._mark(tok, reads, writes)
        return tok

    def barrier(self):
        toks = [(E["sem"], E["n"], n) for n, E in self.E.items() if E["n"] > 0]
        toks += [t for q in self.dtoks.values() for t in q if t is not None]
        for e in self.E:
            for t in toks:
                if t[2] == e:
                    continue
                self._wait(e, t)


def build(debug=False, stages=9):
    nc = bass.Bass("TRN2", target_bir_lowering=False)

    def din(name, shape, dt=F32):
        return nc.dram_tensor(name, shape, dt, kind="ExternalInput").ap()

    xT = din("xT", [128, 8, NTOK])
    c_fm = din("c_fm", [128, 8])
    bada = din("bada", [128, 72])
    lng = din("lng", [128, 24])
    lnb = din("lnb", [128, 24])
    psc = din("psc", [128, 4])
    w_ada = din("w_ada", [D, 9 * D])
    f1g = din("f1g", [D, DFF])
    f1u = din("f1u", [D, DFF])
    f1d = din("f1d", [DFF, D])
    w_in = din("w_in", [D, INC])
    w_pool = din("w_pool", [4, 128, 128])
    w_a = din("w_a", [512, D])
    w_b = din("w_b", [512, D])
    w_out = din("w_out", [D, D])
    f2g = din("f2g", [D, DFF])
    f2u = din("f2u", [D, DFF])
    f2d = din("f2d", [DFF, D])
    sp_in = din("sp_in", [128, 8, 3, 128])
    im_in = din("im_in", [128, 2, 128])
    b31_in = din("b31_in", [128, 8])
    hval_in = din("hval_in", [128, 256])
    corr_in = din("corr_in", [128, 4, 128])
    outT = nc.dram_tensor("outT", [128, 8, NOWN], F32, kind="ExternalOutput").ap()
    okind = "ExternalOutput" if debug else "Internal"
    x1sp = nc.dram_tensor("x1sp", [128, 8, NOWN], F32, kind=okind).ap()
    u2sp = nc.dram_tensor("u2sp", [128, 8, NOWN], BF16, kind=okind).ap()
    if debug:
        dbgK = nc.dram_tensor("dbgK", [128, 4, 4096], BF16, kind="ExternalOutput").ap()
        dbgV = nc.dram_tensor("dbgV", [128, 32, 520], BF16, kind="ExternalOutput").ap()
        dbgS = nc.dram_tensor("dbgS", [128, 4096], F32, kind="ExternalOutput").ap()
        dbgM = nc.dram_tensor("dbgM", [128, 72], F32, kind="ExternalOutput").ap()

    w_ada_r = w_ada.rearrange("(k p) n -> p k n", p=128)
    w_in_r = w_in.rearrange("(k p) n -> p k n", p=128)
    w_out_r = w_out.rearrange("(k p) n -> p k n", p=128)

    with ExitStack() as es:
        S = Sched(nc, es)

        def sbt(stack, name, shape, dt):
            return stack.enter_context(nc.sbuf_tensor(name, shape, dt))

        banks = [es.enter_context(nc.psum_tensor(f"bank{i}", [128, 512], F32)) for i in range(7)]
        ptb = es.enter_context(nc.psum_tensor("ptb", [128, 1024], BF16))
        Rb = [Res() for _ in range(7)]
        Rpt = Res()

        modv = sbt(es, "modv", [128, 72], F32)
        vecs = sbt(es, "vecs", [128, 12, 8], F32)
        lng_t = sbt(es, "lng_t", [128, 24], F32)
        lnb_t = sbt(es, "lnb_t", [128, 24], F32)
        psc_t = sbt(es, "psc_t", [128, 4], F32)
        onesM = sbt(es, "onesM", [128, 128], BF16)
        ident = sbt(es, "ident", [128, 128], BF16)
        phalo = sbt(es, "phalo", [128, 4, 256], F32)
        Rc = Res()
        Rphalo = Res()
        V_A1, V_SH1, V_C1, V_G2, V_B2, V_C2, V_G3, V_B3, V_C3, V_T0, V_T1, V_T2 = range(12)

        S.dma("sp", lng_t[:], lng, writes=[Rc])
        S.dma("sp", lnb_t[:], lnb, writes=[Rc])
        S.dma("sp", psc_t[:], psc, writes=[Rc])
        S.op("dve", lambda h: h.memset(onesM[:], 1.0 / 1024.0), writes=[Rc])
        S.op("pool", lambda h: h.memset(ident[:], 1.0), writes=[Rc])
        S.op("pool", lambda h: h.affine_select(out=ident[:], in_=ident[:], pattern=[[-1, 128]],
                                               compare_op=ALU.is_equal, fill=0.0, base=0,
                                               channel_multiplier=1), reads=[Rc], writes=[Rc])

        with ExitStack() as s0:
            cf = sbt(s0, "cf", [128, 8], F32)
            csl = sbt(s0, "csl", [128, 8], BF16)
            bad = sbt(s0, "bad", [128, 72], F32)
            wab = [sbt(s0, f"wab{i}", [128, 8, 1024], BF16) for i in range(2)]
            Rwab = [Res(), Res()]
            Rcf = Res()
            S.dma("sp", cf[:], c_fm, writes=[Rcf])
            S.dma("sp", bad[:], bada, writes=[Rcf])
            S.op("act", lambda h: h.activation(out=csl[:], in_=cf[:], func=AF.Silu), reads=[Rcf], writes=[Rcf])
            for v in range(9):
                wb = wab[v % 2]
                S.dma("pool", wb[:], w_ada_r[:, :, v * 1024:(v + 1) * 1024], writes=[Rwab[v % 2]])
                for c in range(8):
                    j = v * 8 + c
                    for k in range(8):
                        S.op("pe", lambda h, wb=wb, c=c, k=k, j=j: h.matmul(
                            banks[6][:, j:j + 1], wb[:, k, c * 128:(c + 1) * 128], csl[:, k:k + 1],
                            start=(k == 0), stop=(k == 7)), reads=[Rwab[v % 2], Rcf], writes=[Rb[6]])
            S.op("dve", lambda h: h.tensor_tensor(out=modv[:], in0=banks[6][:, 0:72], in1=bad[:], op=ALU.add),
                 reads=[Rb[6], Rcf], writes=[Rc])

            def mv(i):
                return modv[:, i * 8:(i + 1) * 8]

            def vv(i):
                return vecs[:, i, :]

            def dv(fn, *a):
                S.op("dve", fn, reads=[Rc], writes=[Rc])
            dv(lambda h: h.tensor_scalar_add(out=vv(V_A1), in0=mv(1), scalar1=1.0))
            dv(lambda h: h.tensor_copy(out=vv(V_SH1), in_=mv(0)))
            dv(lambda h: h.tensor_scalar_mul(out=vv(V_C1), in0=mv(2), scalar1=0.5 / ALPHA))
            dv(lambda h: h.tensor_scalar_add(out=vv(V_T0), in0=mv(4), scalar1=1.0))
            dv(lambda h: h.tensor_tensor(out=vv(V_G2), in0=lng_t[:, 0:8], in1=vv(V_T0), op=ALU.mult))
            dv(lambda h: h.tensor_tensor(out=vv(V_T1), in0=lnb_t[:, 0:8], in1=vv(V_T0), op=ALU.mult))
            dv(lambda h: h.tensor_tensor(out=vv(V_B2), in0=vv(V_T1), in1=mv(3), op=ALU.add))
            dv(lambda h: h.tensor_scalar_mul(out=vv(V_C2), in0=mv(5), scalar1=1.0 / ALPHA))
            dv(lambda h: h.tensor_scalar_add(out=vv(V_T2), in0=mv(7), scalar1=1.0))
            dv(lambda h: h.tensor_tensor(out=vv(V_G3), in0=lng_t[:, 8:16], in1=vv(V_T2), op=ALU.mult))
            dv(lambda h: h.tensor_tensor(out=vv(V_T1), in0=lnb_t[:, 8:16], in1=vv(V_T2), op=ALU.mult))
            dv(lambda h: h.tensor_tensor(out=vv(V_B3), in0=vv(V_T1), in1=mv(6), op=ALU.add))
            dv(lambda h: h.tensor_scalar_mul(out=vv(V_C3), in0=mv(8), scalar1=0.5 / ALPHA))
            if debug:
                S.dma("sp", dbgM, modv[:], reads=[Rc])
            S.barrier()

        def vcol(i, c):
            return vecs[:, i, c:c + 1]

        def make_ffn_ctx(stack, pf):
            ctx = {}
            ctx["WA"] = [sbt(stack, pf + f"WA{i}", [128, 8, 128], BF16) for i in range(8)]
            ctx["RWA"] = [Res() for _ in range(8)]
            ctx["wa_n"] = 0
            ctx["WD"] = [sbt(stack, pf + f"WD{i}", [128, NF, 128], BF16) for i in range(3)]
            ctx["RWD"] = [Res() for _ in range(3)]
            ctx["wd_n"] = 0
            ctx["hT"] = sbt(stack, pf + "hT", [128, NF, 512], BF16)
            ctx["Rh"] = [Res() for _ in range(NF)]
            ctx["z"] = sbt(stack, pf + "z", [128, 8, 512], F32)
            ctx["Rz"] = [Res() for _ in range(8)]
            ctx["sg"] = [sbt(stack, pf + f"sg{i}", [128, 512], F32) for i in range(2)]
            ctx["Rsg"] = [Res(), Res()]
            ctx["sg_n"] = 0
            ctx["zb"] = [sbt(stack, pf + f"zb{i}", [128, 512], BF16) for i in range(2)]
            ctx["zs"] = [sbt(stack, pf + f"zs{i}", [128, 512], BF16) for i in range(2)]
            ctx["Rzb"] = [Res(), Res()]
            ctx["Rzs"] = [Res(), Res()]
            ctx["st"] = [sbt(stack, pf + f"st{i}", [128, 512], F32) for i in range(3)]
            ctx["Rst"] = Res()
            ctx["pb"] = 0
            return ctx

        def wa_load(ctx, src_ap):
            i = ctx["wa_n"] % 8
            ctx["wa_n"] += 1
            S.dma("pool", ctx["WA"][i][:], src_ap, writes=[ctx["RWA"][i]])
            return ctx["WA"][i], ctx["RWA"][i]

        def next_bank(ctx, lo, n):
            b = lo + (ctx["pb"] % n)
            ctx["pb"] += 1
            return b

        def ffn(ctx, uT, Ru, W, wg, wu, wd, cvec, resid_fn):
            hT, Rh, z, Rz = ctx["hT"], ctx["Rh"], ctx["z"], ctx["Rz"]
            wg_r = wg.rearrange("(k p) n -> p k n", p=128)
            wu_r = wu.rearrange("(k p) n -> p k n", p=128)
            wd_r = wd.rearrange("(k p) n -> p k n", p=128)
            for f in range(NF):
                tg, Rg = wa_load(ctx, wg_r[:, :, f * 128:(f + 1) * 128])
                tu, Ruu = wa_load(ctx, wu_r[:, :, f * 128:(f + 1) * 128])
                bg = f % 2
                bu = 2 + f % 2
                for k in range(8):
                    S.op("pe", lambda h, k=k: h.matmul(banks[bg][:, :W], tg[:, k, :], uT[:, k, :W],
                                                      start=(k == 0), stop=(k == 7)),
                         reads=[Rg, Ru], writes=[Rb[bg]])
                for k in range(8):
                    S.op("pe", lambda h, k=k: h.matmul(banks[bu][:, :W], tu[:, k, :], uT[:, k, :W],
                                                      start=(k == 0), stop=(k == 7)),
                         reads=[Ruu, Ru], writes=[Rb[bu]])
                si = ctx["sg_n"] % 2
                ctx["sg_n"] += 1
                sg, Rsg = ctx["sg"][si], ctx["Rsg"][si]
                S.op("act", lambda h: h.activation(out=sg[:, :W], in_=banks[bg][:, :W], func=AF.Silu),
                     reads=[Rb[bg]], writes=[Rsg])
                S.op("dve", lambda h: h.tensor_tensor(out=hT[:, f, :W], in0=banks[bu][:, :W], in1=sg[:, :W],
                                                      op=ALU.mult),
                     reads=[Rb[bu], Rsg], writes=[Rh[f]])
            for c in range(8):
                i = ctx["wd_n"] % 3
                ctx["wd_n"] += 1
                wdt, Rwd = ctx["WD"][i], ctx["RWD"][i]
                S.dma("pool", wdt[:], wd_r[:, :, c * 128:(c + 1) * 128], writes=[Rwd])
                resid_fn(c)
                by = 4 + c % 2
                for k in range(NF):
                    S.op("pe", lambda h, k=k: h.matmul(banks[by][:, :W], wdt[:, k, :], hT[:, k, :W],
                                                      start=(k == 0), stop=(k == NF - 1)),
                         reads=[Rwd, Rh[k]], writes=[Rb[by]])
                S.op("dve", lambda h: h.scalar_tensor_tensor(out=z[:, c, :W], in0=banks[by][:, :W],
                                                             scalar=vcol(cvec, c), in1=z[:, c, :W],
                                                             op0=ALU.mult, op1=ALU.add),
                     reads=[Rb[by], Rz[c], Rc], writes=[Rz[c]])

        def layernorm(ctx, W, outs):
            z, Rz = ctx["z"], ctx["Rz"]
            st, Rst = ctx["st"], ctx["Rst"]
            bm, bq = 6, 0
            for c in range(8):
                i = c % 2
                S.op("act", lambda h: h.activation(out=ctx["zb"][i][:, :W], in_=z[:, c, :W], func=AF.Copy),
                     reads=[Rz[c]], writes=[ctx["Rzb"][i]])
                S.op("act", lambda h: h.activation(out=ctx["zs"][i][:, :W], in_=z[:, c, :W], func=AF.Square),
                     reads=[Rz[c]], writes=[ctx["Rzs"][i]])
                S.op("pe", lambda h: h.matmul(banks[bm][:, :W], onesM[:], ctx["zb"][i][:, :W],
                                              start=(c == 0), stop=(c == 7)),
                     reads=[ctx["Rzb"][i], Rc], writes=[Rb[bm]])
                S.op("pe", lambda h: h.matmul(banks[bq][:, :W], onesM[:], ctx["zs"][i][:, :W],
                                              start=(c == 0), stop=(c == 7)),
                     reads=[ctx["Rzs"][i], Rc], writes=[Rb[bq]])
            S.op("act", lambda h: h.activation(out=st[0][:, :W], in_=banks[bm][:, :W], func=AF.Square),
                 reads=[Rb[bm]], writes=[Rst])
            S.op("dve", lambda h: h.tensor_tensor(out=st[0][:, :W], in0=banks[bq][:, :W], in1=st[0][:, :W],
                                                  op=ALU.subtract), reads=[Rb[bq], Rst], writes=[Rst])
            S.op("dve", lambda h: h.tensor_scalar_add(out=st[1][:, :W], in0=st[0][:, :W], scalar1=EPS_P),
                 reads=[Rst], writes=[Rst])
            S.op("act", lambda h: h.activation(out=st[1][:, :W], in_=st[1][:, :W], func=AF.Sqrt),
                 reads=[Rst], writes=[Rst])
            S.op("dve", lambda h: h.reciprocal(out=st[1][:, :W], in_=st[1][:, :W]), reads=[Rst], writes=[Rst])
            S.op("dve", lambda h: h.tensor_tensor(out=st[2][:, :W], in0=banks[bm][:, :W], in1=st[1][:, :W],
                                                  op=ALU.mult), reads=[Rb[bm], Rst], writes=[Rst])
            for c in range(8):
                S.op("dve", lambda h: h.tensor_tensor(out=z[:, c, :W], in0=z[:, c, :W], in1=st[1][:, :W],
                                                      op=ALU.mult), reads=[Rz[c], Rst], writes=[Rz[c]])
                S.op("dve", lambda h: h.tensor_tensor(out=z[:, c, :W], in0=z[:, c, :W], in1=st[2][:, :W],
                                                      op=ALU.subtract), reads=[Rz[c], Rst], writes=[Rz[c]])
                for f in outs:
                    f(c)

        attn_sp = nc.dram_tensor("attn_sp", [128, 4, NOWN], BF16, kind=okind).ap()
        q_sp = nc.dram_tensor("q_sp", [128, 4, NOWN], BF16, kind="Internal").ap()
        qi_sp = nc.dram_tensor("qi_sp", [128, 4, NOWN], BF16, kind="Internal").ap()
        with ExitStack() as skv:
            kT = sbt(skv, "kT", [128, 4, 4096], BF16)
            vS = sbt(skv, "vS", [128, 32, 8, 65], BF16)
            kiT = sbt(skv, "kiT", [128, 4096], BF16)
            RkT, RvS, RkiT = Res(), Res(), Res()
            S.op("dve", lambda h: h.memset(vS[:, :, :, 64:65], 1.0), writes=[RvS])

            with ExitStack() as s1:
                ctx = make_ffn_ctx(s1, "a_")
                u1T = sbt(s1, "u1T", [128, 8, 512], BF16)
                Ru1 = Res()
                u2T = sbt(s1, "u2T", [128, 8, 512], BF16)
                Ru2 = [Res() for _ in range(8)]
                xt = [sbt(s1, f"xt{i}", [128, 512], F32) for i in range(3)]
                Rxt = [Res() for _ in range(3)]
                x1t = [sbt(s1, f"x1t{i}", [128, 512], F32) for i in range(2)]
                Rx1t = [Res(), Res()]
                WV = sbt(s1, "WV", [128, 8, 512], BF16)
                RWV = Res()
                hval = sbt(s1, "hval", [128, 256], F32)
                Rhv = Res()
                S.dma("pool", WV[:], w_in_r[:, :, 1536:2048], writes=[RWV])
                S.dma("sp", hval[:], hval_in, writes=[Rhv])
                ntiles = 9 if stages >= 2 else 1
                xn = 0
                for ti in range(ntiles):
                    t0 = ti * 512
                    W = 512 if ti < 8 else 256
                    own = ti < 4
                    halo = ti == 8
                    for c in range(8):
                        xi = xn % 3
                        xn += 1
                        S.dma("sp", xt[xi][:, :W], xT[:, c, t0:t0 + W], writes=[Rxt[xi]])
                        S.op("act", lambda h: h.activation(out=u1T[:, c, :W], in_=xt[xi][:, :W], func=AF.Identity,
                                                           scale=vcol(V_A1, c), bias=vcol(V_SH1, c)),
                             reads=[Rxt[xi], Rc], writes=[Ru1])

                    def resid1(c):
                        S.dma("sp", ctx["z"][:, c, :W], xT[:, c, t0:t0 + W], writes=[ctx["Rz"][c]])

                    ffn(ctx, u1T, Ru1, W, f1g, f1u, f1d, V_C1, resid1)

                    def out_x1(c):
                        if not own:
                            return
                        i = c % 2
                        S.op("dve", lambda h: h.tensor_scalar(out=x1t[i][:, :W], in0=ctx["z"][:, c, :W],
                                                              scalar1=lng_t[:, c:c + 1], scalar2=lnb_t[:, c:c + 1],
                                                              op0=ALU.mult, op1=ALU.add),
                             reads=[ctx["Rz"][c], Rc], writes=[Rx1t[i]])
                        S.dma("sp", x1sp[:, c, t0:t0 + W], x1t[i][:, :W], reads=[Rx1t[i]])

                    def out_u2(c):
                        S.op("act", lambda h: h.activation(out=u2T[:, c, :W], in_=ctx["z"][:, c, :W], func=AF.Identity,
                                                           scale=vcol(V_G2, c), bias=vcol(V_B2, c)),
                             reads=[ctx["Rz"][c], Rc], writes=[Ru2[c]])

                    layernorm(ctx, W, [out_x1, out_u2])
                    if own:
                        S.dma("sp", u2sp[:, :, t0:t0 + W], u2T[:, :, :W], reads=Ru2)
                    if not halo:
                        for c in range(4):
                            wt, Rw = wa_load(ctx, w_in_r[:, :, 1024 + c * 128:1024 + (c + 1) * 128])
                            b = 4 + c % 2
                            for k in range(8):
                                S.op("pe", lambda h, k=k: h.matmul(banks[b][:, :W], wt[:, k, :], u2T[:, k, :W],
                                                                  start=(k == 0), stop=(k == 7)),
                                     reads=[Rw, Ru2[k]], writes=[Rb[b]])
                            S.op("act", lambda h: h.activation(out=kT[:, c, t0:t0 + W], in_=banks[b][:, :W], func=AF.Copy),
                                 reads=[Rb[b]], writes=[RkT])
                        i = ctx["wa_n"] % 8
                        ctx["wa_n"] += 1
                        wt, Rw = ctx["WA"][i], ctx["RWA"][i]
                        S.dma("pool", wt[:, :, 0:64], w_in_r[:, :, 2560:2624], writes=[Rw])
                        S.dma("pool", wt[:, :, 64:128], w_in_r[:, :, 2560:2624], writes=[Rw])
                        b = 4
                        for k in range(8):
                            S.op("pe", lambda h, k=k: h.matmul(banks[b][:, :W], wt[:, k, :], u2T[:, k, :W],
                                                              start=(k == 0), stop=(k == 7)),
                                 reads=[Rw, Ru2[k]], writes=[Rb[b]])
                        S.op("act", lambda h: h.activation(out=kiT[:, t0:t0 + W], in_=banks[b][:, :W], func=AF.Copy),
                             reads=[Rb[b]], writes=[RkiT])
                        for sbk in range(W // 128):
                            b = 2 + sbk % 2
                            for k in range(8):
                                S.op("pe", lambda h, k=k: h.matmul(banks[b][:, :], u2T[:, k, sbk * 128:(sbk + 1) * 128],
                                                                  WV[:, k, :], start=(k == 0), stop=(k == 7)),
                                     reads=[RWV, Ru2[k]], writes=[Rb[b]])
                            blk = t0 // 128 + sbk
                            S.op("dve", lambda h: h.tensor_copy(out=vS[:, blk, :, 0:64],
                                                                in_=banks[b][:, :].rearrange("p (h d) -> p h d", h=8)),
                                 reads=[Rb[b]], writes=[RvS])
                    else:
                        for g in range(4):
                            wt, Rw = wa_load(ctx, w_in_r[:, :, g * 128:(g + 1) * 128])
                            b = 4 + g % 2
                            for k in range(8):
                                S.op("pe", lambda h, k=k: h.matmul(banks[b][:, :W], wt[:, k, :], u2T[:, k, :W],
                                                                  start=(k == 0), stop=(k == 7)),
                                     reads=[Rw, Ru2[k]], writes=[Rb[b]])
                            S.op("dve", lambda h: h.tensor_tensor(out=phalo[:, g, :], in0=banks[b][:, :W], in1=hval[:],
                                                                  op=ALU.mult),
                                 reads=[Rb[b], Rhv], writes=[Rphalo])
                if debug:
                    S.dma("sp", dbgK, kT[:], reads=[RkT])
                    S.dma("sp", dbgV, vS[:].rearrange("p b h d -> p b (h d)"), reads=[RvS])
                S.barrier()

            if stages >= 3:
                with ExitStack() as s2:
                    stage2a(nc, S, s2, sbt, banks, ptb, Rb, Rpt, dict(
                        kT=kT, vS=vS, kiT=kiT, RkT=RkT, RvS=RvS, RkiT=RkiT, attn_sp=attn_sp, q_sp=q_sp, qi_sp=qi_sp,
                        u2sp=u2sp, w_in_r=w_in_r, sp_in=sp_in, im_in=im_in, b31_in=b31_in, ident=ident, Rc=Rc,
                        dbgS=(dbgS if debug else None), stages=stages))
                    S.barrier()
        S.barrier()

        if stages >= 5:
            with ExitStack() as s3:
                ctx = make_ffn_ctx(s3, "b_")
                u2t = sbt(s3, "u2t", [128, 8, 512], BF16)
                Ru2t = Res()
                WPL = sbt(s3, "WPL", [128, 4, 128], BF16)
                WAr = sbt(s3, "WAr", [128, 4, 1024], BF16)
                WBr = sbt(s3, "WBr", [128, 4, 1024], BF16)
                Rwres = Res()
                corr = sbt(s3, "corr", [128, 4, 128], F32)
                pb0 = sbt(s3, "pb0", [128, 4, 144], F32)
                pbA = sbt(s3, "pbA", [128, 4, 144], F32)
                pbB = sbt(s3, "pbB", [128, 4, 144], F32)
                Rpb = Res()
                pooled = sbt(s3, "pooled", [128, 512], BF16)
                Rpooled = Res()
                mixT = sbt(s3, "mixT", [128, 4, 512], BF16)
                Rmix = Res()
                mrgT = sbt(s3, "mrgT", [128, 8, 512], BF16)
                Rmrg = [Res() for _ in range(8)]
                sgt = [sbt(s3, f"sgt{i}", [128, 512], F32) for i in range(2)]
                Rsgt = [Res(), Res()]
                m1t = sbt(s3, "m1t", [128, 512], F32)
                Rm1 = Res()
                x2 = sbt(s3, "x2", [128, 8, 512], F32)
                Rx2 = [Res() for _ in range(8)]
                u3T = sbt(s3, "u3T", [128, 8, 512], BF16)
                Ru3 = Res()
                attnT = sbt(s3, "attnT", [128, 4, 512], BF16)
                RattnT = Res()
                ot = [sbt(s3, f"ot{i}", [128, 512], F32) for i in range(2)]
                Rot = [Res(), Res()]
                S.dma("pool", WPL[:], w_pool.rearrange("g c d -> c g d"), writes=[Rwres])
                S.dma("pool", WAr[:], w_a.rearrange("(k p) n -> p k n", p=128), writes=[Rwres])
                S.dma("pool", WBr[:], w_b.rearrange("(k p) n -> p k n", p=128), writes=[Rwres])
                S.dma("sp", corr[:], corr_in, writes=[Rwres])
                out_toks = []
                W = 512
                for it in range(4 if stages >= 6 else 1):
                    t0 = it * 512
                    S.dma("sp", u2t[:], u2sp[:, :, t0:t0 + W], writes=[Ru2t])
                    S.dma("sp", attnT[:], attn_sp[:, :, t0:t0 + W], writes=[RattnT])
                    for g in range(4):
                        wt, Rw = wa_load(ctx, w_in_r[:, :, g * 128:(g + 1) * 128])
                        b = 4 + g % 2
                        for k in range(8):
                            S.op("pe", lambda h, k=k: h.matmul(banks[b][:, :W], wt[:, k, :], u2t[:, k, :],
                                                              start=(k == 0), stop=(k == 7)),
                                 reads=[Rw, Ru2t], writes=[Rb[b]])
                        S.op("dve", lambda h: h.tensor_copy(out=pb0[:, :, 0:16],
                                                            in_=phalo[:, g, it * 64:(it + 1) * 64].rearrange("p (b t) -> p b t", b=4)),
                             reads=[Rphalo], writes=[Rpb])
                        S.op("dve", lambda h: h.tensor_copy(out=pb0[:, :, 16:144],
                                                            in_=banks[b][:, :].rearrange("p (b t) -> p b t", b=4)),
                             reads=[Rb[b]], writes=[Rpb])
                        src = pb0
                        dsts = [pbA, pbB]
                        for i in range(g + 1):
                            sh = 1 << i
                            dst = dsts[i % 2]
                            S.op("dve", lambda h, src=src, dst=dst, sh=sh: h.tensor_tensor(
                                out=dst[:, :, sh:144], in0=src[:, :, sh:144], in1=src[:, :, 0:144 - sh], op=ALU.add),
                                reads=[Rpb], writes=[Rpb])
                            src = dst
                        wwin = float(1 << (g + 1))
                        S.op("dve", lambda h, src=src: h.tensor_scalar_mul(out=src[:, :, 16:144], in0=src[:, :, 16:144],
                                                                         scalar1=1.0 / wwin), reads=[Rpb], writes=[Rpb])
                        if it == 0:
                            S.op("dve", lambda h, src=src: h.tensor_tensor(out=src[:, 0, 16:144], in0=src[:, 0, 16:144],
                                                                         in1=corr[:, g, :], op=ALU.mult),
                                 reads=[Rpb, Rwres], writes=[Rpb])
                        S.op("dve", lambda h, src=src: h.tensor_tensor(out=pooled[:].rearrange("p (b t) -> p b t", b=4),
                                                                     in0=src[:, :, 16:144], in1=pb0[:, :, 16:144],
                                                                     op=ALU.subtract),
                             reads=[Rpb], writes=[Rpooled])
                        b2 = 2 + g % 2
                        S.op("pe", lambda h: h.matmul(banks[b2][:, :W], WPL[:, g, :], pooled[:], start=True, stop=True),
                             reads=[Rwres, Rpooled], writes=[Rb[b2]])
                        S.op("act", lambda h: h.activation(out=mixT[:, g, :], in_=banks[b2][:, :W], func=AF.Copy,
                                                           scale=psc_t[:, g:g + 1]),
                             reads=[Rb[b2], Rc], writes=[Rmix])
                    for c in range(8):
                        for br in range(2):
                            wres = WAr if br == 0 else WBr
                            rhs_t = mixT if br == 0 else attnT
                            gcol = 2632 + br * 1024 + c * 128
                            wt, Rw = wa_load(ctx, w_in_r[:, :, gcol:gcol + 128])
                            by = 4 + br
                            bgt = 2 + br
                            for k in range(4):
                                rhs = rhs_t[:, k, :]
                                S.op("pe", lambda h, k=k, rhs=rhs: h.matmul(banks[by][:, :W], wres[:, k, c * 128:(c + 1) * 128], rhs,
                                                                            start=(k == 0), stop=(k == 3)),
                                     reads=[Rwres, Rmix, RattnT], writes=[Rb[by]])
                            for k in range(8):
                                S.op("pe", lambda h, k=k: h.matmul(banks[bgt][:, :W], wt[:, k, :], u2t[:, k, :],
                                                                  start=(k == 0), stop=(k == 7)),
                                     reads=[Rw, Ru2t], writes=[Rb[bgt]])
                            S.op("act", lambda h: h.activation(out=sgt[br][:], in_=banks[bgt][:, :W], func=AF.Sigmoid),
                                 reads=[Rb[bgt]], writes=[Rsgt[br]])
                            if br == 0:
                                S.op("dve", lambda h: h.tensor_tensor(out=m1t[:], in0=banks[by][:, :W], in1=sgt[0][:], op=ALU.mult),
                                     reads=[Rb[by], Rsgt[0]], writes=[Rm1])
                            else:
                                S.op("dve", lambda h: h.tensor_tensor(out=sgt[1][:], in0=banks[by][:, :W], in1=sgt[1][:], op=ALU.mult),
                                     reads=[Rb[by], Rsgt[1]], writes=[Rsgt[1]])
                                S.op("dve", lambda h: h.tensor_tensor(out=mrgT[:, c, :], in0=m1t[:], in1=sgt[1][:], op=ALU.add),
                                     reads=[Rm1, Rsgt[1]], writes=[Rmrg[c]])
                    for c in range(8):
                        wt, Rw = wa_load(ctx, w_out_r[:, :, c * 128:(c + 1) * 128])
                        S.dma("sp", ctx["z"][:, c, :], x1sp[:, c, t0:t0 + W], writes=[ctx["Rz"][c]])
                        by = 4 + c % 2
                        for k in range(8):
                            S.op("pe", lambda h, k=k: h.matmul(banks[by][:, :W], wt[:, k, :], mrgT[:, k, :],
                                                              start=(k == 0), stop=(k == 7)),
                                 reads=[Rw, Rmrg[k]], writes=[Rb[by]])
                        S.op("dve", lambda h: h.scalar_tensor_tensor(out=ctx["z"][:, c, :], in0=banks[by][:, :W],
                                                                     scalar=vcol(V_C2, c), in1=ctx["z"][:, c, :],
                                                                     op0=ALU.mult, op1=ALU.add),
                             reads=[Rb[by], ctx["Rz"][c], Rc], writes=[ctx["Rz"][c]])

                    def out_x2(c):
                        S.op("dve", lambda h: h.tensor_scalar(out=x2[:, c, :], in0=ctx["z"][:, c, :],
                                                              scalar1=lng_t[:, 8 + c:9 + c], scalar2=lnb_t[:, 8 + c:9 + c],
                                                              op0=ALU.mult, op1=ALU.add),
                             reads=[ctx["Rz"][c], Rc], writes=[Rx2[c]])

                    def out_u3(c):
                        S.op("act", lambda h: h.activation(out=u3T[:, c, :], in_=ctx["z"][:, c, :], func=AF.Identity,
                                                           scale=vcol(V_G3, c), bias=vcol(V_B3, c)),
                             reads=[ctx["Rz"][c], Rc], writes=[Ru3])

                    layernorm(ctx, W, [out_x2, out_u3])

                    def resid3(c):
                        S.op("act", lambda h: h.activation(out=ctx["z"][:, c, :], in_=x2[:, c, :], func=AF.Copy),
                             reads=[Rx2[c]], writes=[ctx["Rz"][c]])

                    ffn(ctx, u3T, Ru3, W, f2g, f2u, f2d, V_C3, resid3)

                    def out_fin(c):
                        i = c % 2
                        S.op("dve", lambda h: h.tensor_scalar(out=ot[i][:], in0=ctx["z"][:, c, :],
                                                              scalar1=lng_t[:, 16 + c:17 + c], scalar2=lnb_t[:, 16 + c:17 + c],
                                                              op0=ALU.mult, op1=ALU.add),
                             reads=[ctx["Rz"][c], Rc], writes=[Rot[i]])
                        out_toks.append(S.dma("sp", outT[:, c, t0:t0 + W], ot[i][:], reads=[Rot[i]]))

                    layernorm(ctx, W, [out_fin])
                S.barrier()
        S.barrier()
        for q in S.dtoks.values():
            for t in q:
                S._wait("sp", t)
    return nc


def stage2a(nc, S, s2, sbt, banks, ptb, Rb, Rpt, P):
    kT, vS, kiT = P["kT"], P["vS"], P["kiT"]
    RkT, RvS, RkiT = P["RkT"], P["RvS"], P["RkiT"]
    attn_sp = P["attn_sp"]
    w_in_r = P["w_in_r"]
    ident, Rc = P["ident"], P["Rc"]
    stages = P["stages"]
    SP = sbt(s2, "SP", [128, 8, 3, 128], F32)
    IM = sbt(s2, "IM", [128, 2, 128], F32)
    b31 = sbt(s2, "b31", [128, 8], F32)
    Rk = Res()
    S.dma("sp", SP[:], P["sp_in"], writes=[Rk])
    S.dma("sp", IM[:], P["im_in"], writes=[Rk])
    S.dma("sp", b31[:], P["b31_in"], writes=[Rk])
    q_sp, qi_sp = P["q_sp"], P["qi_sp"]
    qblk = [sbt(s2, f"qblk{i}", [128, 4, 128], BF16) for i in range(4)]
    Rqblk = [Res() for _ in range(4)]
    qiblk = [sbt(s2, f"qiblk{i}", [128, 4, 128], BF16) for i in range(3)]
    Rqiblk = [Res() for _ in range(3)]
    wtm = sbt(s2, "wtm", [128, 16, 8], F32)
    Rwtm = Res()
    cnt = {"bk": 0, "bi": 0, "bl": 0, "ar": 0, "pt": 0, "tb": 0, "qs": 0}

    def bank():
        rot = (0, 1, 2, 3)
        b = rot[cnt["bk"] % len(rot)]
        cnt["bk"] += 1
        return b

    def bank_idx():
        b = (3, 6)[cnt["bi"] % 2]
        cnt["bi"] += 1
        return b

    def bank_log():
        b = (0, 1, 2)[cnt["bl"] % 3]
        cnt["bl"] += 1
        return b

    nblocks = 16 if stages >= 4 else 1
    with ExitStack() as sq:
        WQ = [sbt(sq, f"WQ{i}", [128, 8, 128], BF16) for i in range(8)]
        RWQ = Res()
        WWI = sbt(sq, "WWI", [128, 8, 8], BF16)
        for c in range(4):
            S.dma("pool", WQ[c][:], w_in_r[:, :, 512 + c * 128:512 + (c + 1) * 128], writes=[RWQ])
            S.dma("pool", WQ[4 + c][:], w_in_r[:, :, 2048 + c * 128:2048 + (c + 1) * 128], writes=[RWQ])
        S.dma("pool", WWI[:], w_in_r[:, :, 2624:2632], writes=[RWQ])
        u2t = [sbt(sq, f"u2q{i}", [128, 8, 512], BF16) for i in range(2)]
        Ru2t = [Res(), Res()]
        qst = [sbt(sq, f"qst{i}", [128, 512], BF16) for i in range(3)]
        Rqst = [Res() for _ in range(3)]
        for it in range((nblocks + 3) // 4):
            t0 = it * 512
            ut, Ru = u2t[it % 2], Ru2t[it % 2]
            S.dma("sp", ut[:], P["u2sp"][:, :, t0:t0 + 512], writes=[Ru])
            for wi_, dsp in enumerate((q_sp, qi_sp)):
                for c in range(4):
                    b = bank()
                    for k in range(8):
                        S.op("pe", lambda h, k=k: h.matmul(banks[b][:, :], WQ[wi_ * 4 + c][:, k, :], ut[:, k, :],
                                                          start=(k == 0), stop=(k == 7)),
                             reads=[RWQ, Ru], writes=[Rb[b]])
                    qi_ = cnt["qs"] % 3
                    cnt["qs"] += 1
                    S.op("act", lambda h: h.activation(out=qst[qi_][:], in_=banks[b][:, :], func=AF.Copy),
                         reads=[Rb[b]], writes=[Rqst[qi_]])
                    S.dma("sp", dsp[:, c, t0:t0 + 512], qst[qi_][:], reads=[Rqst[qi_]])
            for jj in range(4):
                j = it * 4 + jj
                for k in range(8):
                    S.op("pe", lambda h, k=k: h.matmul(banks[6][:, jj * 8:jj * 8 + 8], ut[:, k, jj * 128:(jj + 1) * 128],
                                                      WWI[:, k, :], start=(k == 0), stop=(k == 7)),
                         reads=[RWQ, Ru], writes=[Rb[6]])
            S.op("dve", lambda h: h.tensor_copy(out=wtm[:, it * 4:it * 4 + 4, :],
                                                in_=banks[6][:, 0:32].rearrange("p (a b) -> p a b", a=4)),
                 reads=[Rb[6]], writes=[Rwtm])
        S.barrier()

    Sc = [sbt(s2, f"Sc{i}", [128, 4096], F32) for i in range(2)]
    RSc = [Res() for _ in range(2)]
    msk = [sbt(s2, f"msk{i}", [128, 4096], BF16) for i in range(4)]
    Rmsk = [Res() for _ in range(4)]
    mskT = [sbt(s2, f"mskT{i}", [128, 32, 128], BF16) for i in range(2)]
    RmskT = [Res(), Res()]
    ar = [sbt(s2, f"ar{i}", [128, 512], F32) for i in range(3)]
    Rar = [Res() for _ in range(3)]
    pt_ = [sbt(s2, f"pt{i}", [128, 512], BF16) for i in range(5)]
    Rpt_ = [Res() for _ in range(5)]
    tmpb = [sbt(s2, f"tmpb{i}", [128, 128], F32) for i in range(2)]
    Rtmpb = [Res(), Res()]
    smt = [sbt(s2, f"sm{i}", [128, 8], F32) for i in range(3)]
    Rsm = [Res() for _ in range(3)]
    rec = sbt(s2, "rec", [128, 8], F32)
    Rrec = Res()
    abf = sbt(s2, "abf", [128, 8, 64], BF16)
    Rabf = Res()
    atile = [sbt(s2, f"atile{i}", [128, 4, 128], BF16) for i in range(2)]
    Ratile = [Res(), Res()]
    LO = -1.0 / 1024.0
    RNG = 1.0 + 2.0 / 1024.0

    def gen_A(j):
        nk = j + 1
        sc, Rs = Sc[j % 2], RSc[j % 2]
        sm, Rm = smt[j % 2], Rsm[j % 2]
        s = j % 4
        qib, Rqib = qiblk[j % 3], Rqiblk[j % 3]
        S.dma("sp", qib[:], qi_sp[:, :, j * 128:(j + 1) * 128], writes=[Rqib])
        S.dma("sp", qblk[j % 4][:], q_sp[:, :, j * 128:(j + 1) * 128], writes=[Rqblk[j % 4]])
        for base in (0, 2048):
            for cs in range(0, nk * 128, 512):
                wd = min(512, nk * 128 - cs)
                for hh in range(8):
                    po = 64 * (hh % 2)
                    b = bank_idx()
                    S.op("pe", lambda h: h.matmul(banks[b][:, :wd], qib[po:po + 64, hh // 2, :],
                                                  kiT[po:po + 64, base + cs:base + cs + wd], start=True, stop=True),
                         reads=[Rqib, RkiT], writes=[Rb[b]])
                    ai = cnt["ar"] % 3
                    cnt["ar"] += 1
                    S.op("act", lambda h: h.activation(out=ar[ai][:, :wd], in_=banks[b][:, :wd], func=AF.Relu),
                         reads=[Rb[b]], writes=[Rar[ai]])
                    dst = sc[:, base + cs:base + cs + wd]
                    if hh == 0:
                        S.op("dve", lambda h: h.tensor_scalar_mul(out=dst, in0=ar[ai][:, :wd], scalar1=wtm[:, j, 0:1]),
                             reads=[Rar[ai], Rwtm], writes=[Rs])
                    else:
                        S.op("dve", lambda h: h.scalar_tensor_tensor(out=dst, in0=ar[ai][:, :wd], scalar=wtm[:, j, hh:hh + 1],
                                                                      in1=dst, op0=ALU.mult, op1=ALU.add),
                             reads=[Rar[ai], Rwtm, Rs], writes=[Rs])
                    yield
        Sv = sc[:].rearrange("p (a s) -> p a s", a=2)[:, :, 0:nk * 128]
        Mv = msk[s][:].rearrange("p (a s) -> p a s", a=2)[:, :, 0:nk * 128]
        S.op("dve", lambda h: h.tensor_reduce(out=sm[:, 0:1], in_=Sv, axis=AX.XY, op=ALU.max), reads=[Rs], writes=[Rm])
        S.op("dve", lambda h: h.tensor_reduce(out=sm[:, 1:2], in_=Sv, axis=AX.XY, op=ALU.min), reads=[Rs], writes=[Rm])
        yield
        S.op("dve", lambda h: h.scalar_tensor_tensor(out=sm[:, 2:3], in0=sm[:, 0:1], scalar=1e-20, in1=sm[:, 1:2],
                                                     op0=ALU.add, op1=ALU.subtract), reads=[Rm], writes=[Rm])
        S.op("dve", lambda h: h.reciprocal(out=sm[:, 2:3], in_=sm[:, 2:3]), reads=[Rm], writes=[Rm])
        S.op("dve", lambda h: h.scalar_tensor_tensor(out=sm[:, 3:4], in0=sm[:, 1:2], scalar=-1.0, in1=sm[:, 2:3],
                                                     op0=ALU.mult, op1=ALU.mult), reads=[Rm], writes=[Rm])
        S.op("act", lambda h: h.activation(out=Sv, in_=Sv, func=AF.Identity, scale=sm[:, 2:3], bias=sm[:, 3:4]),
             reads=[Rs, Rm], writes=[Rs])
        for mi, base in ((0, 0), (1, 2048)):
            dst = sc[:, base + j * 128:base + (j + 1) * 128]
            S.op("dve", lambda h: h.tensor_tensor(out=dst, in0=dst, in1=IM[:, mi, :], op=ALU.add),
                 reads=[Rs, Rk], writes=[Rs])
        yield
        step = RNG / 2
        if j % 2 == 1:
            n_tot = 2 * nk * 128
            S.op("dve", lambda h: h.memset(sm[:, 4:5], -(LO + step)), reads=[Rm], writes=[Rm])
            cur = 4
            for i in range(NBIS):
                S.op("act", lambda h: h.activation(out=Mv, in_=Sv, func=AF.Sign, bias=sm[:, cur:cur + 1],
                                                   accum_out=sm[:, 6:7]),
                     reads=[Rs, Rm], writes=[Rm, Rmsk[s]])
                S.op("act", lambda h: h.activation(out=sm[:, 7:8], in_=sm[:, 6:7], func=AF.Sign, bias=float(n_tot - 511)),
                     reads=[Rm], writes=[Rm])
                nxt = 9 - cur
                sc_ = -(step / 2)
                S.op("act", lambda h: h.activation(out=sm[:, nxt:nxt + 1], in_=sm[:, 7:8], func=AF.Identity, scale=sc_,
                                                   bias=sm[:, cur:cur + 1]), reads=[Rm], writes=[Rm])
                cur = nxt
                if i < NBIS - 1:
                    step = step / 2
                yield
            S.op("dve", lambda h: h.tensor_scalar(out=Mv, in0=Sv, scalar1=sm[:, cur:cur + 1], scalar2=-(step / 2),
                                                  op0=ALU.add, op1=ALU.is_ge),
                 reads=[Rs, Rm], writes=[Rmsk[s]])
            yield
            return
        S.op("dve", lambda h: h.memset(sm[:, 4:5], LO + step), reads=[Rm], writes=[Rm])
        for i in range(NBIS):
            S.op("dve", lambda h: h.tensor_scalar(out=Mv, in0=Sv, scalar1=sm[:, 4:5], scalar2=0.0,
                                                  op0=ALU.is_ge, op1=ALU.add, accum_out=sm[:, 5:6]),
                 reads=[Rs, Rm], writes=[Rm, Rmsk[s]])
            if i < NBIS - 1:
                nstep = step / 2
                S.op("dve", lambda h: h.tensor_scalar(out=sm[:, 6:7], in0=sm[:, 5:6], scalar1=255.5, scalar2=2.0 * nstep,
                                                      op0=ALU.is_ge, op1=ALU.mult), reads=[Rm], writes=[Rm])
                S.op("dve", lambda h: h.scalar_tensor_tensor(out=sm[:, 4:5], in0=sm[:, 6:7], scalar=-nstep, in1=sm[:, 4:5],
                                                             op0=ALU.add, op1=ALU.add), reads=[Rm], writes=[Rm])
                step = nstep
            else:
                S.op("dve", lambda h: h.tensor_scalar(out=sm[:, 6:7], in0=sm[:, 5:6], scalar1=255.5, scalar2=step,
                                                      op0=ALU.is_lt, op1=ALU.mult), reads=[Rm], writes=[Rm])
                S.op("dve", lambda h: h.tensor_tensor(out=sm[:, 4:5], in0=sm[:, 4:5], in1=sm[:, 6:7], op=ALU.subtract),
                     reads=[Rm], writes=[Rm])
            yield
        S.op("dve", lambda h: h.tensor_scalar(out=Mv, in0=Sv, scalar1=sm[:, 4:5], scalar2=None, op0=ALU.is_ge),
             reads=[Rs, Rm], writes=[Rmsk[s]])
        if P["dbgS"] is not None and j == 3:
            S.dma("sp", P["dbgS"], sc[:], reads=[Rs])
        yield

    def gen_B(j):
        s = j % 4
        nk = j + 1
        mT, RmT = mskT[j % 2], RmskT[j % 2]
        qb, Rqb = qblk[j % 4], Rqblk[j % 4]
        kbs = list(range(0, nk)) + list(range(16, 16 + nk))
        for g0 in range(0, len(kbs), 8):
            grp = kbs[g0:g0 + 8]
            for i, kb in enumerate(grp):
                S.op("pe", lambda h, i=i, kb=kb: h.transpose(ptb[:, i * 128:(i + 1) * 128], msk[s][:, kb * 128:(kb + 1) * 128], ident[:]),
                     reads=[Rmsk[s], Rc], writes=[Rpt])
            runs = []
            for i, kb in enumerate(grp):
                if runs and runs[-1][1] + runs[-1][2] == kb:
                    runs[-1][2] += 1
                else:
                    runs.append([i, kb, 1])
            for (i0, kb0, n) in runs:
                S.op("act", lambda h, i0=i0, kb0=kb0, n=n: h.activation(
                    out=mT[:, kb0:kb0 + n, :], in_=ptb[:, i0 * 128:(i0 + n) * 128].rearrange("p (a t) -> p a t", a=n),
                    func=AF.Copy), reads=[Rpt], writes=[RmT])
            yield
        far = []
        for lo_, hi_ in ((0, j), (16, 16 + j - 1)):
            x = lo_
            while x < hi_:
                n = min(4, hi_ - x)
                far.append((x, n))
                x += n
        specials = [(j, 1), (16 + j, 2)]
        if j >= 1:
            specials.append((16 + j - 1, 0))
        units = []
        for hh in range(8):
            ul = [("far", kb0, n, None) for kb0, n in far] + [("sp", kb, 1, si) for kb, si in specials]
            nun = sum(u[2] for u in ul)
            done = 0
            for kind, kb0, n, si in ul:
                units.append((hh, kind, kb0, n, si, done, nun))
                done += n
        st = {}

        def emit_L(u):
            hh, kind, kb0, n, si, done, nun = units[u]
            po = 64 * (hh % 2)
            b = bank_log()
            for i in range(n):
                kb = kb0 + i
                S.op("pe", lambda h, i=i, kb=kb: h.matmul(banks[b][:, i * 128:(i + 1) * 128],
                                                        kT[po:po + 64, hh // 2, kb * 128:(kb + 1) * 128],
                                                        qb[po:po + 64, hh // 2, :], start=True, stop=True),
                     reads=[RkT, Rqb], writes=[Rb[b]])
            st[u] = b

        def emit_post(u):
            hh, kind, kb0, n, si, done, nun = units[u]
            b = st[u]
            pi = cnt["pt"] % 5
            cnt["pt"] += 1
            pt, Rp = pt_[pi], Rpt_[pi]
            if kind == "far":
                S.op("act", lambda h: h.activation(out=pt[:, :n * 128], in_=banks[b][:, :n * 128], func=AF.Exp,
                                                   scale=0.125, bias=b31[:, hh:hh + 1]),
                     reads=[Rb[b], Rk], writes=[Rp])
            else:
                ti = cnt["tb"] % 2
                cnt["tb"] += 1
                S.op("dve", lambda h: h.scalar_tensor_tensor(out=tmpb[ti][:], in0=banks[b][:, 0:128], scalar=0.125,
                                                             in1=SP[:, hh, si, :], op0=ALU.mult, op1=ALU.add),
                     reads=[Rb[b], Rk], writes=[Rtmpb[ti]])
                S.op("act", lambda h: h.activation(out=pt[:, :128], in_=tmpb[ti][:], func=AF.Exp),
                     reads=[Rtmpb[ti]], writes=[Rp])
            S.op("pool", lambda h: h.tensor_tensor(out=pt[:, :n * 128], in0=pt[:, :n * 128],
                                                   in1=mT[:, kb0:kb0 + n, :].rearrange("p a t -> p (a t)"), op=ALU.mult),
                 reads=[Rp, RmT], writes=[Rp])
            st[u] = (pt, Rp)

        def emit_PV(u):
            hh, kind, kb0, n, si, done, nun = units[u]
            pt, Rp = st[u]
            ob = 4 + hh // 4
            oc = (hh % 4) * 65
            for i in range(n):
                kb = kb0 + i
                S.op("pe", lambda h, i=i, kb=kb: h.matmul(
                    banks[ob][:, oc:oc + 65], pt[:, i * 128:(i + 1) * 128], vS[:, kb, hh, :],
                    start=(done + i == 0), stop=(done + i == nun - 1)),
                    reads=[Rp, RvS], writes=[Rb[ob]])

        NU = len(units)
        LA = 2
        for u in range(min(LA, NU)):
            emit_L(u)
        LAG = 2
        for u in range(NU + LAG):
            if u < NU:
                emit_post(u)
                if u + LA < NU:
                    emit_L(u + LA)
            if u - LAG >= 0:
                emit_PV(u - LAG)
            yield
        for half in range(2):
            ov = banks[4 + half][:, 0:260].rearrange("p (h d) -> p h d", h=4)
            S.op("dve", lambda h: h.reciprocal(out=rec[:, half * 4:half * 4 + 4], in_=ov[:, :, 64]),
                 reads=[Rb[4 + half]], writes=[Rrec])
            for h4 in range(4):
                hh = half * 4 + h4
                S.op("dve", lambda h, hh=hh, h4=h4: h.tensor_scalar_mul(out=abf[:, hh, :], in0=ov[:, h4, 0:64],
                                                                      scalar1=rec[:, hh:hh + 1]),
                     reads=[Rb[4 + half], Rrec], writes=[Rabf])
        for c in range(4):
            S.op("pe", lambda h, c=c: h.transpose(ptb[:, c * 128:(c + 1) * 128],
                                                  abf[:, 2 * c:2 * c + 2, :].rearrange("p h d -> p (h d)"), ident[:]),
                 reads=[Rabf, Rc], writes=[Rpt])
        S.op("act", lambda h: h.activation(out=atile[j % 2][:], in_=ptb[:, 0:512].rearrange("p (c t) -> p c t", c=4), func=AF.Copy),
             reads=[Rpt], writes=[Ratile[j % 2]])
        S.dma("sp", attn_sp[:, :, j * 128:(j + 1) * 128], atile[j % 2][:], reads=[Ratile[j % 2]])
        yield

    def n_yields_A(j):
        nk = j + 1
        per_range = sum(8 for _ in range(0, nk * 128, 512))
        return 2 * per_range + 2 + NBIS + 1

    def run_all(gens_):
        alive = list(gens_)
        while alive:
            for g in list(alive):
                try:
                    next(g)
                except StopIteration:
                    alive.remove(g)

    def n_yields_B(j):
        nk = j + 1
        return (2 * nk + 7) // 8 + 8 * (len(range(0, j, 4)) + len(range(0, max(j - 1, 0), 4)) + (3 if j >= 1 else 2)) + 3

    run_all([gen_A(j) for j in range(min(2, nblocks))])
    for p in range((nblocks + 1) // 2):
        bl = [j for j in (2 * p, 2 * p + 1) if j < nblocks]
        al = [j for j in (2 * p + 2, 2 * p + 3) if j < nblocks]
        gas = [gen_A(j) for j in al]
        nas = [n_yields_A(j) + 1 for j in al]
        nB = sum(n_yields_B(j) for j in bl)
        acc = [0.0 for _ in al]
        alive = [True for _ in al]
        for j in bl:
            for _ in gen_B(j):
                for i in range(len(al)):
                    acc[i] += nas[i] / nB
                    while acc[i] >= 1.0 and alive[i]:
                        acc[i] -= 1.0
                        try:
                            next(gas[i])
                        except StopIteration:
                            alive[i] = False
        run_all([g for g, a in zip(gas, alive) if a])


def _t5_bucket(n):
    n = np.asarray(n)
    nf = np.maximum(n, 1).astype(np.float32)
    large = 16 + (np.log(nf / np.float32(16)) / np.float32(math.log(128 / 16)) * np.float32(16)).astype(np.int32)
    large = np.minimum(large, 31)
    return np.where(n < 16, n, large)


def _fm(v):
    v = np.asarray(v, np.float32).reshape(-1, 8, 128)
    return np.ascontiguousarray(v.transpose(2, 0, 1).reshape(128, -1))


_NC_CACHE = {}


def prep_core(core, x, c, b_ada, ln_g, ln_b, pool_scale, rel_bias):
    b, r = core // 2, core % 2
    own_blocks = [2 * j + r for j in range(16)]
    oth_blocks = [2 * j + 1 - r for j in range(16)]
    idx = []
    for g in own_blocks:
        idx += list(range(g * 128, (g + 1) * 128))
    for g in oth_blocks:
        idx += list(range(g * 128, (g + 1) * 128))
    hv = np.ones(256, np.float32)
    for j, g in enumerate(own_blocks):
        for i in range(16):
            t = g * 128 - 16 + i
            if t < 0:
                hv[j * 16 + i] = 0.0
                t = 0
            idx.append(t)
    idx = np.asarray(idx)
    xs = x[b][idx]
    xTl = np.ascontiguousarray(xs.T.reshape(8, 128, NTOK).transpose(1, 0, 2))
    s_i = np.arange(128)[:, None]
    t_i = np.arange(128)[None, :]
    sp = np.zeros((128, 8, 3, 128), np.float32)
    d_own = np.maximum(t_i - s_i, 0)
    sp[:, :, 1, :] = rel_bias[_t5_bucket(d_own)].transpose(0, 2, 1)
    d_near = 128 + t_i - s_i
    near = rel_bias[_t5_bucket(d_near)].transpose(0, 2, 1)
    far = np.broadcast_to(rel_bias[31][None, :, None], (128, 8, 128))
    if r == 1:
        sp[:, :, 2, :] = near
        sp[:, :, 0, :] = far
    else:
        sp[:, :, 2, :] = 0.0
        sp[:, :, 0, :] = near
    im = np.zeros((128, 2, 128), np.float32)
    tt = np.arange(128)[:, None]
    ss = np.arange(128)[None, :]
    im[:, 0, :] = np.where(ss <= tt, 0.0, NEG)
    im[:, 1, :] = 0.0 if r == 1 else NEG
    b31 = np.ascontiguousarray(np.broadcast_to(rel_bias[31][None, :], (128, 8))).astype(np.float32)
    corr = np.ones((128, 4, 128), np.float32)
    if r == 0:
        for g in range(4):
            w = 1 << (g + 1)
            tpos = np.arange(128)
            corr[:, g, :] = (w / np.minimum(tpos + 1, w)).astype(np.float32)[None, :]
    return {
        "xT": xTl,
        "c_fm": _fm(c[b]),
        "bada": _fm(b_ada[0]),
        "lng": _fm(ln_g[0].reshape(-1)),
        "lnb": _fm(ln_b[0].reshape(-1)),
        "psc": np.ascontiguousarray(pool_scale[0].reshape(4, 128).T),
        "sp_in": sp, "im_in": im, "b31_in": b31,
        "hval_in": np.ascontiguousarray(np.broadcast_to(hv[None, :], (128, 256))),
        "corr_in": corr,
    }


def kernel(x, c, w_ada, b_ada, ln_g, ln_b, ffn1_w_gate, ffn1_w_up, ffn1_w_down,
           w_in, w_pool, pool_scale, w_a, w_b, w_out, rel_bias,
           ffn2_w_gate, ffn2_w_up, ffn2_w_down):
    A = lambda a: np.ascontiguousarray(np.asarray(a, dtype=np.float32))
    x, c, b_ada, ln_g, ln_b, pool_scale, rel_bias = map(A, (x, c, b_ada, ln_g, ln_b, pool_scale, rel_bias))
    shared = {
        "w_ada": A(w_ada)[0], "f1g": A(ffn1_w_gate)[0], "f1u": A(ffn1_w_up)[0], "f1d": A(ffn1_w_down)[0],
        "w_in": A(w_in)[0], "w_pool": A(w_pool)[0], "w_a": A(w_a)[0], "w_b": A(w_b)[0], "w_out": A(w_out)[0],
        "f2g": A(ffn2_w_gate)[0], "f2u": A(ffn2_w_up)[0], "f2d": A(ffn2_w_down)[0],
    }
    if "nc" not in _NC_CACHE:
        _NC_CACHE["nc"] = build(debug=False, stages=9)
    nc = _NC_CACHE["nc"]
    in_maps = []
    for core in range(8):
        m = dict(shared)
        m.update(prep_core(core, x, c, b_ada, ln_g, ln_b, pool_scale, rel_bias))
        in_maps.append(m)
    res = run_bass_kernel_spmd(nc, in_maps, core_ids=list(range(8)))
    out = np.zeros((4, S_LEN, D), np.float32)
    for core in range(8):
        b, r = core // 2, core % 2
        o = res.results[core]["outT"]
        o = o.transpose(2, 1, 0).reshape(NOWN, D)
        for j in range(16):
            g = 2 * j + r
            out[b, g * 128:(g + 1) * 128] = o[j * 128:(j + 1) * 128]
    return out
```
